# Optimizing a Trainium2 kernel written in Bass

```python
import math
import jax, jax.numpy as jnp
from jax import lax
import numpy as np

D_MODEL = 1024
BATCH = 2
SEQ = 16384
DEPTH = 2

HEAD_DIM = 64
MIX_WIDTH = D_MODEL
N_HEADS_TOTAL = MIX_WIDTH // HEAD_DIM
N_HEADS_DIFF = N_HEADS_TOTAL // 4
N_HEADS_DIL = (N_HEADS_TOTAL - N_HEADS_DIFF) // 2
N_HEADS_NA = N_HEADS_TOTAL - N_HEADS_DIFF - N_HEADS_DIL
DIFF_QK_DIM = HEAD_DIM // 2
DIL_PATTERNS = ((128, 1), (512, 4), (2048, 16))
GRID_W = 64
NA_KH = 8
NA_KW = 16
D_FF = 2816
CONV_W = 3
ROPE_THETA = 10000.0
Q_BLOCK = 128
EPS = 1e-6
NEG = -1e30

kernel_name = "hybrid_diff_dilated_neighborhood_encoder"


def rms_norm(x, g):
    xf = x.astype(jnp.float32)
    y = xf * lax.rsqrt(jnp.mean(xf * xf, axis=-1, keepdims=True) + EPS)
    return y.astype(x.dtype) * g


def rope(x, pos):
    half = x.shape[-1] // 2
    inv = ROPE_THETA ** (-jnp.arange(half, dtype=jnp.float32) / half)
    ang = pos[..., None] * inv
    cos, sin = jnp.cos(ang), jnp.sin(ang)
    x1 = x[..., :half].astype(jnp.float32)
    x2 = x[..., half:].astype(jnp.float32)
    return jnp.concatenate([x1 * cos - x2 * sin, x2 * cos + x1 * sin], axis=-1).astype(x.dtype)


def diff_attention(q, k, v, diff_lambda, subln_g, layer_idx):
    B, H, S, _, DQK = q.shape
    DH = v.shape[-1]
    lam_init = 0.8 - 0.6 * math.exp(-0.3 * layer_idx)
    lq1, lk1, lq2, lk2 = diff_lambda[0], diff_lambda[1], diff_lambda[2], diff_lambda[3]
    lam = (jnp.exp(jnp.sum(lq1 * lk1).astype(jnp.float32))
           - jnp.exp(jnp.sum(lq2 * lk2).astype(jnp.float32)) + lam_init)
    scale = DQK ** -0.5
    n_blk = S // Q_BLOCK
    qb = q.reshape(B, H, n_blk, Q_BLOCK, 2, DQK).transpose(2, 0, 1, 3, 4, 5)

    def block(qi):
        s = jnp.einsum('bhqmd,bhkmd->bhmqk', qi, k).astype(jnp.float32) * scale
        p = jax.nn.softmax(s, axis=-1)
        a = p[:, :, 0] - lam * p[:, :, 1]
        return jnp.einsum('bhqk,bhkd->bhqd', a.astype(v.dtype), v)

    o = lax.map(block, qb)
    o = o.transpose(1, 2, 0, 3, 4).reshape(B, H, S, DH)
    return rms_norm(o, subln_g) * (1.0 - lam_init)


def banded_window_attention(q, k, v, r):
    lead = q.shape[:-2]
    L, DH = q.shape[-2:]
    n = -(-L // r)
    padl = [(0, 0)] * len(lead)
    qb = jnp.pad(q, padl + [(0, n * r - L), (0, 0)]).reshape(*lead, n, r, DH)

    def key_blocks(t):
        tp = jnp.pad(t, padl + [(r, (n + 1) * r - L), (0, 0)]).reshape(*lead, n + 2, r, DH)
        return jnp.concatenate([tp[..., :-2, :, :], tp[..., 1:-1, :, :], tp[..., 2:, :, :]], axis=-2)

    kb, vb = key_blocks(k), key_blocks(v)
    qi = jnp.arange(r)[:, None]
    kj = jnp.arange(3 * r)[None, :]
    key_pos = jnp.arange(n)[:, None, None] * r - r + kj
    mask = (kj >= qi) & (kj <= qi + 2 * r) & (key_pos >= 0) & (key_pos < L)
    s = jnp.einsum('...nqd,...nkd->...nqk', qb, kb).astype(jnp.float32) * (DH ** -0.5)
    s = jnp.where(mask, s, NEG)
    m = jnp.max(s, axis=-1, keepdims=True)
    p = jnp.exp(s - m)
    den = jnp.sum(p, axis=-1)
    o = jnp.einsum('...nqk,...nkd->...nqd', (p / den[..., None]).astype(v.dtype), vb)
    lse = m[..., 0] + jnp.log(den)
    return o.reshape(*lead, n * r, DH)[..., :L, :], lse.reshape(*lead, n * r)[..., :L]


def dilated_attention(q, k, v):
    B, H, S, DH = q.shape
    outs, lses = [], []
    for window, dil in DIL_PATTERNS:
        r = window // (2 * dil)
        L = S // dil

        def split(t):
            return t.reshape(B, H, L, dil, DH).transpose(0, 1, 3, 2, 4)

        o, lse = banded_window_attention(split(q), split(k), split(v), r)
        outs.append(o.transpose(0, 1, 3, 2, 4).reshape(B, H, S, DH))
        lses.append(lse.transpose(0, 1, 3, 2).reshape(B, H, S))
    w = jax.nn.softmax(jnp.stack(lses, axis=0), axis=0)
    out = jnp.sum(w[..., None] * jnp.stack(outs, axis=0).astype(jnp.float32), axis=0)
    return out.astype(q.dtype)


def neighborhood_attention(q, k, v, rpb):
    B, H, S, DH = q.shape
    rows = S // GRID_W
    kh = min(NA_KH, rows)
    kw = NA_KW
    qg = q.reshape(B, H, rows, GRID_W, DH)
    kg = k.reshape(B, H, rows, GRID_W, DH)
    vg = v.reshape(B, H, rows, GRID_W, DH)
    r = jnp.arange(rows)
    row_start = jnp.clip(r - kh // 2, 0, rows - kh)
    key_rows = row_start[:, None] + jnp.arange(kh)[None, :]
    kr = kg[:, :, key_rows]
    vr = vg[:, :, key_rows]
    col = jnp.arange(GRID_W)
    col_start = jnp.clip(col - kw // 2, 0, GRID_W - kw)
    col_mask = (col[None, :] >= col_start[:, None]) & (col[None, :] < col_start[:, None] + kw)
    dr = key_rows - r[:, None] + (NA_KH - 1)
    dc = jnp.clip(col[None, :] - col[:, None], -(kw - 1), kw - 1) + (NA_KW - 1)
    bias = rpb[:, dr[:, None, :, None], dc[None, :, None, :]]
    s = jnp.einsum('bhrqd,bhrjkd->bhrqjk', qg, kr).astype(jnp.float32) * (DH ** -0.5)
    s = s + bias[None].astype(jnp.float32)
    s = jnp.where(col_mask[:, None, :], s, NEG)
    p = jax.nn.softmax(s.reshape(B, H, rows, GRID_W, kh * GRID_W), axis=-1)
    o = jnp.einsum('bhrqn,bhrnd->bhrqd', p.astype(v.dtype), vr.reshape(B, H, rows, kh * GRID_W, DH))
    return o.reshape(B, H, S, DH)


def hybrid_mixer(h, w_in, diff_lambda, diff_subln, na_rpb, w_out, layer_idx):
    B, S, _ = h.shape
    proj = h @ w_in
    wa, wb, wc = N_HEADS_DIFF * HEAD_DIM, N_HEADS_DIL * HEAD_DIM, N_HEADS_NA * HEAD_DIM
    offs = np.cumsum([wa, wa, wa, wb, wb, wb, wc, wc, wc])[:-1].tolist()
    qa, ka, va, qb, kb, vb, qc, kc, vc = jnp.split(proj, offs, axis=-1)
    pos = jnp.arange(S, dtype=jnp.float32)

    def heads(t, n_h):
        return t.reshape(B, S, n_h, HEAD_DIM).transpose(0, 2, 1, 3)

    def diff_heads(t):
        return t.reshape(B, S, N_HEADS_DIFF, 2, DIFF_QK_DIM).transpose(0, 2, 1, 3, 4)

    def merge(o):
        return o.transpose(0, 2, 1, 3).reshape(B, S, -1)

    o_a = diff_attention(rope(diff_heads(qa), pos[:, None]), rope(diff_heads(ka), pos[:, None]),
                         heads(va, N_HEADS_DIFF), diff_lambda, diff_subln, layer_idx)
    o_b = dilated_attention(rope(heads(qb, N_HEADS_DIL), pos), rope(heads(kb, N_HEADS_DIL), pos),
                            heads(vb, N_HEADS_DIL))
    o_c = neighborhood_attention(heads(qc, N_HEADS_NA), heads(kc, N_HEADS_NA), heads(vc, N_HEADS_NA), na_rpb)
    o = jnp.concatenate([merge(o_a), merge(o_b), merge(o_c)], axis=-1)
    return o @ w_out


def conv_glu_ffn(h, w_up, conv_w, conv_b, w_down):
    S = h.shape[1]
    g, u = jnp.split(h @ w_up, 2, axis=-1)
    pad = CONV_W // 2
    gp = jnp.pad(g, ((0, 0), (pad, pad), (0, 0)))
    gc = conv_b
    for j in range(CONV_W):
        gc = gc + gp[:, j:j + S] * conv_w[j]
    return (jax.nn.silu(gc) * u) @ w_down


def setup_inputs(seed: int = 0) -> dict:
    key = jax.random.key(seed)
    ks = jax.random.split(key, 16)
    D = D_MODEL

    def nrm(k, shape, s):
        return jax.random.normal(k, shape, jnp.float32) * s

    return {
        "x": nrm(ks[0], (BATCH, SEQ, D), 1.0),
        "c": nrm(ks[1], (BATCH, D), 1.0),
        "w_ada": nrm(ks[2], (DEPTH, D, 6 * D), 0.5 * D ** -0.5),
        "b_ada": nrm(ks[3], (DEPTH, 6 * D), 0.02),
        "g_attn": 1.0 + nrm(ks[4], (DEPTH, D), 0.02),
        "w_in": nrm(ks[5], (DEPTH, D, 3 * MIX_WIDTH), D ** -0.5),
        "diff_lambda": nrm(ks[6], (DEPTH, 4, DIFF_QK_DIM), 0.1),
        "diff_subln": 1.0 + nrm(ks[7], (DEPTH, HEAD_DIM), 0.02),
        "na_rpb": nrm(ks[8], (DEPTH, N_HEADS_NA, 2 * NA_KH - 1, 2 * NA_KW - 1), 0.1),
        "w_out": nrm(ks[9], (DEPTH, MIX_WIDTH, D), MIX_WIDTH ** -0.5),
        "g_ffn": 1.0 + nrm(ks[10], (DEPTH, D), 0.02),
        "w_up": nrm(ks[11], (DEPTH, D, 2 * D_FF), D ** -0.5),
        "conv_w": nrm(ks[12], (DEPTH, CONV_W, D_FF), CONV_W ** -0.5),
        "conv_b": nrm(ks[13], (DEPTH, D_FF), 0.02),
        "w_down": nrm(ks[14], (DEPTH, D_FF, D), D_FF ** -0.5),
        "g_final": 1.0 + nrm(ks[15], (D,), 0.02),
    }


def reference(x, c, w_ada, b_ada, g_attn, w_in, diff_lambda, diff_subln, na_rpb, w_out,
              g_ffn, w_up, conv_w, conv_b, w_down, g_final):
    for l in range(DEPTH):
        mod = jax.nn.silu(c) @ w_ada[l] + b_ada[l]
        sh_a, sc_a, gt_a, sh_f, sc_f, gt_f = [m[:, None, :] for m in jnp.split(mod, 6, axis=-1)]
        h = rms_norm(x, g_attn[l]) * (1.0 + sc_a) + sh_a
        x = x + gt_a * hybrid_mixer(h, w_in[l], diff_lambda[l], diff_subln[l], na_rpb[l], w_out[l], l)
        h = rms_norm(x, g_ffn[l]) * (1.0 + sc_f) + sh_f
        x = x + gt_f * conv_glu_ffn(h, w_up[l], conv_w[l], conv_b[l], w_down[l])
    return rms_norm(x, g_final)
```

```python
import contextlib
import math
import numpy as np
import ml_dtypes
import concourse.bass as bass
import concourse.mybir as mybir
from concourse.bass_utils import run_bass_kernel_spmd

F32 = mybir.dt.float32
BF16 = mybir.dt.bfloat16
ALU = mybir.AluOpType
AF = mybir.ActivationFunctionType
NPBF = ml_dtypes.bfloat16

D = 1024
S = 16384
B = 2
DEPTH = 2
NCORE = 8
TQ = 4096
NBLK = TQ // 512
DFF = 2816
EPS = 1e-6


class Sched:
    COMPUTE = ("pe", "act", "dve", "pool")

    def __init__(self, nc):
        self.nc = nc
        self.ops = []
        self.res = {}
        self.slot_cnt = {}

    def op(self, eng, fn, reads=(), writes=(), slot=None):
        o = dict(eng=eng, fn=fn, reads=tuple(reads), writes=tuple(writes), slot=slot,
                 deps=[], signal=False, idx=len(self.ops))
        stream = {"poolq": "pool", "actq": "act"}.get(eng, eng)
        o["stream"] = stream
        o["is_dma"] = eng in ("sync", "poolq", "actq")
        deps = []
        for r in o["reads"]:
            st = self.res.setdefault(r, dict(w=[], r=[]))
            deps += st["w"]
        for w in o["writes"]:
            st = self.res.setdefault(w, dict(w=[], r=[]))
            deps += st["w"] + st["r"]
        seen = set()
        for d in deps:
            if d["idx"] in seen:
                continue
            seen.add(d["idx"])
            if (not d["is_dma"]) and d["stream"] == stream and stream == "pe" and not o["is_dma"]:
                continue
            o["deps"].append(d)
            d["signal"] = True
        for r in o["reads"]:
            self.res[r]["r"].append(o)
        for w in o["writes"]:
            self.res[w] = dict(w=[o], r=[])
        if o["is_dma"]:
            assert slot is not None
            o["signal"] = True
        self.ops.append(o)
        return o

    def emit(self, final_wait_stream="sync"):
        nc = self.nc
        sem_names = []
        counters = {}
        for o in self.ops:
            if not o["signal"]:
                continue
            key = ("dma", o["slot"]) if o["is_dma"] else ("eng", o["stream"])
            if key not in counters:
                counters[key] = 0
                sem_names.append(key)
            counters[key] += 16 if o["is_dma"] else 1
            o["ev"] = (key, counters[key])
        final = {}
        for o in self.ops:
            if o["signal"]:
                final[o["ev"][0]] = max(final.get(o["ev"][0], 0), o["ev"][1])
        with contextlib.ExitStack() as es:
            sems = {}
            for i, key in enumerate(sem_names):
                sems[key] = es.enter_context(nc.semaphore("s%d" % i))
            block = es.enter_context(nc.Block())
            streams = {}
            for o in self.ops:
                streams.setdefault(o["stream"], []).append(o)

            def run_stream(name, engobj):
                known = {}
                for o in streams.get(name, []):
                    for d in o["deps"]:
                        key, val = d["ev"]
                        if known.get(key, 0) < val:
                            engobj.wait_ge(sems[key], val)
                            known[key] = val
                    inst = o["fn"](engobj)
                    if o["signal"]:
                        key, val = o["ev"]
                        inst.then_inc(sems[key], 16 if o["is_dma"] else 1)
                        if not o["is_dma"] and False:
                            known[key] = val
                if name == final_wait_stream:
                    for key, val in final.items():
                        if known.get(key, 0) < val:
                            engobj.wait_ge(sems[key], val)

            @block.sync
            def _(e):
                run_stream("sync", e)

            @block.tensor
            def _(e):
                run_stream("pe", e)

            @block.scalar
            def _(e):
                run_stream("act", e)

            @block.vector
            def _(e):
                run_stream("dve", e)

            @block.gpsimd
            def _(e):
                run_stream("pool", e)


def new_nc():
    return bass.Bass("TRN2", target_bir_lowering=False)


ROPE_CHUNKS = [0, 1, 2, 3, 6, 7, 8, 9, 10, 11]


def build_stage_a():
    nc = new_nc()
    xT = nc.dram_tensor("xT", [D, TQ], F32, kind="ExternalInput").ap()
    cT = nc.dram_tensor("cT", [128, 8], F32, kind="ExternalInput").ap()
    wada = nc.dram_tensor("wada", [D, 2048], F32, kind="ExternalInput").ap()
    bada = nc.dram_tensor("bada", [128, 16], F32, kind="ExternalInput").ap()
    gat = nc.dram_tensor("gat", [128, 8], F32, kind="ExternalInput").ap()
    win = nc.dram_tensor("win", [D, 3072], F32, kind="ExternalInput").ap()
    cosd = nc.dram_tensor("cosd", [128, TQ], F32, kind="ExternalInput").ap()
    sind = nc.dram_tensor("sind", [128, TQ], F32, kind="ExternalInput").ap()
    cosl = nc.dram_tensor("cosl", [128, TQ], F32, kind="ExternalInput").ap()
    sinl = nc.dram_tensor("sinl", [128, TQ], F32, kind="ExternalInput").ap()
    qkvT = nc.dram_tensor("qkvT", [3072, TQ], BF16, kind="ExternalOutput").ap()

    sc = Sched(nc)
    with contextlib.ExitStack() as es:
        def sb(name, shape, dt):
            return es.enter_context(nc.sbuf_tensor(name, shape, dt))

        def ps(name, shape, dt=F32):
            return es.enter_context(nc.psum_tensor(name, shape, dt))

        wbf = sb("wbf", [128, 8, 3072], BF16)
        wrot = sb("wrot", [128, 8, 1280], BF16)
        wtmp = [sb("wtmp%d" % i, [128, 1536], F32) for i in range(2)]
        small = sb("small", [128, 64], F32)
        ones = sb("ones", [128, 128], F32)
        xb = [sb("xb%d" % i, [128, 8, 512], F32) for i in range(2)]
        xsq = sb("xsq", [128, 8, 512], F32)
        rstd = sb("rstd", [128, 512], F32)
        htmp = [sb("htmp%d" % i, [128, 512], F32) for i in range(2)]
        hT = sb("hT", [128, 8, 512], BF16)
        tabs = [sb("tabs%d" % i, [128, 4, 512], F32) for i in range(2)]
        r1 = [sb("r1_%d" % i, [128, 512], F32) for i in range(2)]
        r2 = [sb("r2_%d" % i, [128, 512], F32) for i in range(2)]
        ob = [sb("ob%d" % i, [128, 512], BF16) for i in range(4)]
        p_mod = ps("p_mod", [128, 16])
        p_ms = ps("p_ms", [128, 512])
        p_q = [ps("p_q%d" % i, [128, 512]) for i in range(2)]
        p_r = [ps("p_r%d" % i, [128, 512]) for i in range(2)]

        c_ap = small[:, 0:8]
        silu_ap = small[:, 8:16]
        mod_ap = small[:, 16:32]
        bada_ap = small[:, 32:48]
        gat_ap = small[:, 48:56]
        gsc_ap = small[:, 56:64]
        sh_ap = small[:, 16:24]

        sc.op("sync", lambda e: e.dma_start(out=c_ap, in_=cT), writes=["cT"], slot="cT")
        sc.op("sync", lambda e: e.dma_start(out=bada_ap, in_=bada), writes=["bada"], slot="bada")
        sc.op("sync", lambda e: e.dma_start(out=gat_ap, in_=gat), writes=["gat"], slot="gat")
        sc.op("pool", lambda e: e.memset(ones[:, :], 1.0 / D), writes=["ones"])
        epsc = sb("epsc", [128, 1], F32)
        eps_ap = epsc[:, 0:1]
        sc.op("pool", lambda e: e.memset(epsc[:, :], EPS), writes=["epsc"])
        sc.op("act", lambda e: e.activation(out=silu_ap, in_=c_ap, func=AF.Silu), reads=["cT"], writes=["silu"])

        xparts = [["xb%dk%d" % (i, k) for k in range(8)] for i in range(2)]
        for jq in range(4):
            wa = xb[jq % 2]
            for kc in range(8):
                sc.op("sync" if kc % 2 == 0 else "poolq",
                      lambda e, kc=kc, wa=wa, jq=jq: e.dma_start(out=wa[:, kc, :], in_=wada[kc * 128:(kc + 1) * 128, jq * 512:(jq + 1) * 512]),
                      writes=[xparts[jq % 2][kc]], slot="xb%d" % (jq % 2))
            for jj in range(4):
                j = jq * 4 + jj
                for kc in range(8):
                    sc.op("pe", lambda e, j=j, jj=jj, kc=kc, wa=wa: e.matmul(p_mod[:, j:j + 1], lhsT=wa[:, kc, jj * 128:(jj + 1) * 128],
                                                                            rhs=silu_ap[:, kc:kc + 1], start=(kc == 0), stop=(kc == 7)),
                          reads=xparts[jq % 2] + ["silu"], writes=["p_mod"])
        sc.op("dve", lambda e: e.tensor_tensor(out=mod_ap, in0=p_mod[:, :], in1=bada_ap, op=ALU.add),
              reads=["p_mod", "bada"], writes=["mod"])
        sc.op("dve", lambda e: e.scalar_tensor_tensor(out=gsc_ap, in0=small[:, 24:32], scalar=1.0, in1=gat_ap,
                                                      op0=ALU.add, op1=ALU.mult),
              reads=["mod", "gat"], writes=["gsc"])

        for kc in range(8):
            for hf in range(2):
                t = wtmp[hf]
                tnm = "wtmp%d" % hf
                c_lo = hf * 1536
                sc.op("sync" if hf == 0 else "poolq",
                      lambda e, kc=kc, t=t, c_lo=c_lo: e.dma_start(out=t[:, :], in_=win[kc * 128:(kc + 1) * 128, c_lo:c_lo + 1536]),
                      writes=[tnm], slot=tnm)
                sc.op("act", lambda e, kc=kc, t=t, c_lo=c_lo: e.activation(out=wbf[:, kc, c_lo:c_lo + 1536], in_=t[:, :], func=AF.Copy),
                      reads=[tnm], writes=["wbf%d_%d" % (kc, hf)])
                if hf == 0:
                    for (c0, n, half, r0) in ((0, 512, 16, 0), (768, 768, 32, 512)):
                        src = t[:, c0:c0 + n].rearrange("p (g h d) -> p g h d", h=2, d=half)
                        dst = wrot[:, kc, r0:r0 + n].rearrange("p (g h d) -> p g h d", h=2, d=half)
                        sc.op("dve", lambda e, src=src, dst=dst: e.tensor_scalar(out=dst[:, :, 0, :], in0=src[:, :, 1, :],
                                                                                 scalar1=-1.0, scalar2=None, op0=ALU.mult),
                              reads=[tnm], writes=["wrot%d_%da" % (kc, c0)])
                        sc.op("pool", lambda e, src=src, dst=dst: e.tensor_copy(out=dst[:, :, 1, :], in_=src[:, :, 0, :]),
                              reads=[tnm], writes=["wrot%d_%db" % (kc, c0)])

        wnames = (["wbf%d_%d" % (k, h) for k in range(8) for h in range(2)]
                  + ["wrot%d_%d%s" % (k, c0, ab) for k in range(8) for c0 in (0, 768) for ab in "ab"])
        obi = 0
        for blk in range(NBLK):
            t0 = blk * 512
            x_ = xb[blk % 2]
            tb = tabs[blk % 2]
            xn = "xb%d" % (blk % 2)
            tn = "tabs%d" % (blk % 2)
            xp = xparts[blk % 2]
            tp = ["%sk%d" % (tn, i) for i in range(4)]
            for kc in range(8):
                sc.op("sync" if kc % 2 == 0 else "poolq",
                      lambda e, kc=kc, x_=x_, t0=t0: e.dma_start(out=x_[:, kc, :], in_=xT[kc * 128:(kc + 1) * 128, t0:t0 + 512]),
                      writes=[xp[kc]], slot=xn)
            for i, tab in enumerate((cosd, sind, cosl, sinl)):
                sc.op("poolq", lambda e, i=i, tab=tab, tb=tb, t0=t0: e.dma_start(out=tb[:, i, :], in_=tab[:, t0:t0 + 512]),
                      writes=[tp[i]], slot=tn)
            sc.op("act", lambda e, x_=x_: e.activation(out=xsq[:, :, :], in_=x_[:, :, :], func=AF.Square),
                  reads=xp, writes=["xsq"])
            for kc in range(8):
                sc.op("pe", lambda e, kc=kc: e.matmul(p_ms[:, :], lhsT=ones[:, :], rhs=xsq[:, kc, :], start=(kc == 0), stop=(kc == 7)),
                      reads=["xsq", "ones"], writes=["p_ms"])
            sc.op("act", lambda e: e.activation(out=rstd[:, :], in_=p_ms[:, :], func=AF.Sqrt, bias=eps_ap, scale=1.0),
                  reads=["p_ms", "epsc"], writes=["rstd0"])
            sc.op("dve", lambda e: e.reciprocal(out=rstd[:, :], in_=rstd[:, :]),
                  reads=["rstd0"], writes=["rstd"])
            for kc in range(8):
                ht = htmp[kc % 2]
                hn = "htmp%d" % (kc % 2)
                sc.op("dve", lambda e, kc=kc, ht=ht, x_=x_: e.scalar_tensor_tensor(out=ht[:, :], in0=x_[:, kc, :], scalar=gsc_ap[:, kc:kc + 1],
                                                                                   in1=rstd[:, :], op0=ALU.mult, op1=ALU.mult),
                      reads=xp + ["rstd", "gsc"], writes=[hn])
                sc.op("pool", lambda e, kc=kc, ht=ht: e.tensor_scalar(out=hT[:, kc, :], in0=ht[:, :], scalar1=sh_ap[:, kc:kc + 1], scalar2=None, op0=ALU.add),
                      reads=[hn, "mod"], writes=["hT%d" % kc])
            hnames = ["hT%d" % k for k in range(8)]
            for oc in range(24):
                pq = p_q[oc % 2]
                pqn = "p_q%d" % (oc % 2)
                for kc in range(8):
                    sc.op("pe", lambda e, kc=kc, oc=oc, pq=pq: e.matmul(pq[:, :], lhsT=wbf[:, kc, oc * 128:(oc + 1) * 128], rhs=hT[:, kc, :],
                                                                        start=(kc == 0), stop=(kc == 7)),
                          reads=hnames + wnames, writes=[pqn])
                o_ = ob[obi % 4]
                on = "ob%d" % (obi % 4)
                obi += 1
                if oc in ROPE_CHUNKS:
                    ri = ROPE_CHUNKS.index(oc)
                    pr = p_r[ri % 2]
                    prn = "p_r%d" % (ri % 2)
                    for kc in range(8):
                        sc.op("pe", lambda e, kc=kc, ri=ri, pr=pr: e.matmul(pr[:, :], lhsT=wrot[:, kc, ri * 128:(ri + 1) * 128], rhs=hT[:, kc, :],
                                                                            start=(kc == 0), stop=(kc == 7)),
                              reads=hnames + wnames, writes=[prn])
                    ci = 0 if oc < 4 else 2
                    a1 = r1[ri % 2]
                    a2 = r2[ri % 2]
                    sc.op("dve", lambda e, pq=pq, a1=a1, tb=tb, ci=ci: e.tensor_tensor(out=a1[:, :], in0=pq[:, :], in1=tb[:, ci, :], op=ALU.mult),
                          reads=[pqn] + tp, writes=["r1_%d" % (ri % 2)])
                    sc.op("dve", lambda e, pr=pr, a2=a2, tb=tb, ci=ci: e.tensor_tensor(out=a2[:, :], in0=pr[:, :], in1=tb[:, ci + 1, :], op=ALU.mult),
                          reads=[prn] + tp, writes=["r2_%d" % (ri % 2)])
                    sc.op("pool", lambda e, a1=a1, a2=a2, o_=o_: e.tensor_tensor(out=o_[:, :], in0=a1[:, :], in1=a2[:, :], op=ALU.add),
                          reads=["r1_%d" % (ri % 2), "r2_%d" % (ri % 2)], writes=[on])
                else:
                    sc.op("act", lambda e, pq=pq, o_=o_: e.activation(out=o_[:, :], in_=pq[:, :], func=AF.Copy),
                          reads=[pqn], writes=[on])
                sc.op("sync", lambda e, o_=o_, oc=oc, t0=t0: e.dma_start(out=qkvT[oc * 128:(oc + 1) * 128, t0:t0 + 512], in_=o_[:, :]),
                      reads=[on], writes=["qkvT"], slot=on)
        sc.emit()
    return nc


def _pc(v, n):
    return np.ascontiguousarray(np.asarray(v, np.float32).reshape(n, 128).T)


def rope_tables():
    out = {}
    pos = np.arange(S, dtype=np.float32)
    for name, half in (("d", 16), ("l", 32)):
        inv = (np.float32(10000.0) ** (-(np.arange(half, dtype=np.float32)) / np.float32(half))).astype(np.float32)
        ang = (pos[None, :] * inv[:, None]).astype(np.float32)
        cos = np.cos(ang.astype(np.float64)).astype(np.float32)
        sin = np.sin(ang.astype(np.float64)).astype(np.float32)
        reps = 128 // half
        out["cos" + name] = np.tile(cos, (reps, 1))
        out["sin" + name] = np.tile(sin, (reps, 1))
    return out


_ROPE = None


def prep_a(inp, l, xT_cores):
    global _ROPE
    if _ROPE is None:
        _ROPE = rope_tables()
    maps = []
    for core in range(NCORE):
        b, q = divmod(core, 4)
        t0 = q * TQ
        m = {
            "xT": xT_cores[core],
            "cT": _pc(inp["c"][b], 8),
            "wada": np.ascontiguousarray(inp["w_ada"][l][:, 0:2048]),
            "bada": _pc(inp["b_ada"][l][0:2048], 16),
            "gat": _pc(inp["g_attn"][l], 8),
            "win": np.ascontiguousarray(inp["w_in"][l]),
            "cosd": np.ascontiguousarray(_ROPE["cosd"][:, t0:t0 + TQ]),
            "sind": np.ascontiguousarray(_ROPE["sind"][:, t0:t0 + TQ]),
            "cosl": np.ascontiguousarray(_ROPE["cosl"][:, t0:t0 + TQ]),
            "sinl": np.ascontiguousarray(_ROPE["sinl"][:, t0:t0 + TQ]),
        }
        maps.append(m)
    return maps


DILS = (1, 4, 16)
VW = 96


def dil_geom(dil):
    lc = TQ // dil
    kcols = dil * (lc + 128)
    ntile = dil * (lc // 128 + 1)
    return lc, kcols, ntile


def build_stage_b():
    nc = new_nc()
    dq = nc.dram_tensor("dq", [4, 64, TQ], BF16, kind="ExternalInput").ap()
    dk = nc.dram_tensor("dk", [4, 64, S], BF16, kind="ExternalInput").ap()
    dv = nc.dram_tensor("dv", [4, 128, 128 * VW], BF16, kind="ExternalInput").ap()
    dlam = nc.dram_tensor("dlam", [64, 128], F32, kind="ExternalInput").ap()
    dcon = nc.dram_tensor("dcon", [64, 4], F32, kind="ExternalInput").ap()
    selh = nc.dram_tensor("selh", [96, 64], F32, kind="ExternalInput").ap()
    lq, lk, lv = {}, {}, {}
    for dil in DILS:
        lc, kcols, ntile = dil_geom(dil)
        lq[dil] = nc.dram_tensor("lq%d" % dil, [6, 64, TQ], BF16, kind="ExternalInput").ap()
        lk[dil] = nc.dram_tensor("lk%d" % dil, [6, 64, kcols], BF16, kind="ExternalInput").ap()
        lv[dil] = nc.dram_tensor("lv%d" % dil, [6, 128, ntile * VW], BF16, kind="ExternalInput").ap()
    lmask = nc.dram_tensor("lmask", [128, 4 * 1024], BF16, kind="ExternalInput").ap()
    nq = nc.dram_tensor("nq", [6, 64, TQ], BF16, kind="ExternalInput").ap()
    nk = nc.dram_tensor("nk", [6, 64, 71 * 64], BF16, kind="ExternalInput").ap()
    nv = nc.dram_tensor("nv", [6, 64, 71 * VW], BF16, kind="ExternalInput").ap()
    nbias = nc.dram_tensor("nbias", [6, 64, 8 * 512], F32, kind="ExternalInput").ap()
    nmask = nc.dram_tensor("nmask", [64, 8 * 512], F32, kind="ExternalInput").ap()
    oT = nc.dram_tensor("oT", [D, TQ], BF16, kind="ExternalOutput").ap()

    sc = Sched(nc)
    OP = sc.op
    with contextlib.ExitStack() as es:
        def sb(name, shape, dt):
            return es.enter_context(nc.sbuf_tensor(name, shape, dt))

        def ps(name, shape, dt=F32):
            return es.enter_context(nc.psum_tensor(name, shape, dt))

        KH = [sb("KH%d" % i, [128, 8192], BF16) for i in range(2)]
        VH = [sb("VH%d" % i, [128, 8192], BF16) for i in range(2)]
        QB = [sb("QB%d" % i, [64, TQ], BF16) for i in range(2)]
        PT = [sb("PT%d" % i, [128, 1024], BF16) for i in range(2)]
        A01 = [sb("A%d" % i, [128, 512], F32) for i in range(2)]
        W = [sb("W%d" % i, [64, 512], F32) for i in range(6)]
        OB = [sb("OB%d" % i, [64, 512], BF16) for i in range(2)]
        accs = sb("accs", [128, TQ], F32)
        mk = sb("mk", [128, 4 * 1024], BF16)
        E = sb("E", [64, 8 * 512], F32)
        nm = sb("nm", [64, 8 * 512], F32)
        XS = [sb("XS%d" % i, [64, 512], F32) for i in range(2)]
        cst = sb("cst", [128, 256], F32)
        sm = sb("sm", [64, 256], F32)
        sS = [ps("sS%d" % i, [128, 1024]) for i in range(2)]
        acc = [ps("acc%d" % i, [128, 512]) for i in range(2)]
        post = [ps("post%d" % i, [128, 512]) for i in range(2)]

        sel = cst[0:96, 0:64]
        ones64 = cst[0:64, 64:128]
        eps_ap = cst[0:64, 128:129]
        OP("pool", lambda e: e.memset(cst[:, :], 0.0), writes=["sel", "ones64", "epsb"])
        OP("sync", lambda e: e.dma_start(out=sel, in_=selh), writes=["sel"], slot="sel")
        OP("pool", lambda e: e.memset(ones64, 1.0 / 64.0), writes=["ones64"])
        OP("pool", lambda e: e.memset(eps_ap, EPS), writes=["epsb"])
        OP("sync", lambda e: e.dma_start(out=sm[:, 0:128], in_=dlam), writes=["dlam"], slot="dlam")
        OP("sync", lambda e: e.dma_start(out=sm[:, 128:132], in_=dcon), writes=["dcon"], slot="dcon")
        OP("poolq", lambda e: e.dma_start(out=mk[:, :], in_=lmask), writes=["mk"], slot="mk")
        OP("poolq", lambda e: e.dma_start(out=nm[:, :], in_=nmask), writes=["nm"], slot="nm")
        dl = sm[:, 0:128].rearrange("p (a b d) -> p a b d", a=2, b=2)
        pr = sm[:, 136:200].rearrange("p (a d) -> p a d", a=2)
        OP("dve", lambda e: e.tensor_tensor(out=pr, in0=dl[:, :, 0, :], in1=dl[:, :, 1, :], op=ALU.mult),
           reads=["dlam"], writes=["lprod"])
        OP("dve", lambda e: e.reduce_sum(out=sm[:, 200:202], in_=pr, axis=mybir.AxisListType.X),
           reads=["lprod"], writes=["lsum"])
        OP("act", lambda e: e.activation(out=sm[:, 202:204], in_=sm[:, 200:202], func=AF.Exp), reads=["lsum"], writes=["lexp"])
        OP("dve", lambda e: e.scalar_tensor_tensor(out=sm[:, 204:205], in0=sm[:, 203:204], scalar=sm[:, 202:203], in1=sm[:, 128:129],
                                                   op0=ALU.subtract, op1=ALU.subtract),
           reads=["lexp", "dcon"], writes=["nlam"])
        OP("dve", lambda e: e.tensor_tensor(out=sm[:, 205:206], in0=sm[:, 130:131], in1=sm[:, 129:130], op=ALU.mult),
           reads=["dcon"], writes=["gsub"])
        nlam = sm[:, 204:205]
        gsub = sm[:, 205:206]

        oi = [0]

        def finalize(src_o, src_names, rows, qcols, nrm=None):
            o_ = OB[oi[0] % 2]
            on = "OB%d" % (oi[0] % 2)
            oi[0] += 1
            OP("pool", lambda e, o_=o_, src_o=src_o: e.tensor_copy(out=o_[:, :], in_=src_o), reads=src_names, writes=[on])
            OP("poolq", lambda e, o_=o_, rows=rows, qcols=qcols: e.dma_start(out=oT[rows:rows + 64, qcols:qcols + 512], in_=o_[:, :]),
               reads=[on], writes=["oT"], slot=on)

        def swap_recip(src, src_names, pb, wi):
            OP("pe", lambda e, src=src, pb=pb: e.matmul(post[pb][0:64, :], lhsT=sel, rhs=src[0:96, :], start=True, stop=True),
               reads=src_names + ["sel"], writes=["post%d" % pb])
            OP("dve", lambda e, pb=pb, wi=wi: e.reciprocal(out=W[wi][:, :], in_=post[pb][0:64, :]),
               reads=["post%d" % pb], writes=["W%d" % wi])

        SCL_D = 1.0 / math.sqrt(32.0)
        VHv = [VH[i][:, 0:64 * VW].rearrange("p (t w) -> p t w", w=VW) for i in range(2)]
        step = 0
        for h in range(4):
            for hf in range(2):
                OP("sync", lambda e, h=h, hf=hf: e.dma_start(out=KH[hf][0:64, :], in_=dk[h, :, hf * 8192:(hf + 1) * 8192]),
                   writes=["KH%d" % hf], slot="KH%d" % hf)
                OP("sync", lambda e, h=h, hf=hf: e.dma_start(out=VH[hf][:, 0:64 * VW], in_=dv[h, :, hf * 64 * VW:(hf + 1) * 64 * VW]),
                   writes=["VH%d" % hf], slot="VH%d" % hf)
            qb = QB[h % 2]
            qn = "QB%d" % (h % 2)
            OP("poolq", lambda e, h=h, qb=qb: e.dma_start(out=qb[:, :], in_=dq[h, :, :]), writes=[qn], slot=qn)
            for qt in range(NBLK):
                q0 = qt * 512

                def s_mm(kt, step, qb=qb, q0=q0, qn=qn):
                    hf, kk = divmod(kt, 64)
                    s_ = sS[step % 2]
                    sn = "sS%d" % (step % 2)
                    for m in range(2):
                        OP("pe", lambda e, m=m, hf=hf, kk=kk, s_=s_, qb=qb, q0=q0: e.matmul(
                            s_[:, m * 512:(m + 1) * 512], lhsT=KH[hf][32 * m:32 * m + 32, kk * 128:(kk + 1) * 128],
                            rhs=qb[32 * m:32 * m + 32, q0:q0 + 512], start=True, stop=True),
                           reads=["KH%d" % hf, qn], writes=[sn])

                s_mm(0, step)
                for kt in range(128):
                    hf, kk = divmod(kt, 64)
                    s_ = sS[step % 2]
                    sn = "sS%d" % (step % 2)
                    p_ = PT[step % 2]
                    pn = "PT%d" % (step % 2)
                    if kt + 1 < 128:
                        s_mm(kt + 1, step + 1)
                    OP("act", lambda e, s_=s_, p_=p_: e.activation(out=p_[:, :], in_=s_[:, :], func=AF.Exp, scale=SCL_D),
                       reads=[sn], writes=[pn])
                    for m in range(2):
                        OP("pe", lambda e, m=m, hf=hf, kk=kk, p_=p_, kt=kt: e.matmul(
                            acc[m][0:VW, :], lhsT=VHv[hf][:, kk, :], rhs=p_[:, m * 512:(m + 1) * 512],
                            start=(kt == 0), stop=(kt == 127)),
                           reads=["VH%d" % hf, pn], writes=["acc%d" % m])
                    step += 1
                for m in range(2):
                    OP("act", lambda e, m=m: e.activation(out=A01[m][0:VW, :], in_=acc[m][0:VW, :], func=AF.Copy),
                       reads=["acc%d" % m], writes=["A%d" % m])
                    swap_recip(A01[m], ["A%d" % m], m, m)
                OP("dve", lambda e: e.tensor_tensor(out=W[2][:, :], in0=A01[0][0:64, :], in1=W[0][:, :], op=ALU.mult),
                   reads=["A0", "W0"], writes=["W2"])
                OP("pool", lambda e: e.tensor_tensor(out=W[3][:, :], in0=A01[1][0:64, :], in1=W[1][:, :], op=ALU.mult),
                   reads=["A1", "W1"], writes=["W3"])
                OP("dve", lambda e: e.scalar_tensor_tensor(out=W[4][:, :], in0=W[3][:, :], scalar=nlam, in1=W[2][:, :],
                                                           op0=ALU.mult, op1=ALU.add),
                   reads=["W2", "W3", "nlam"], writes=["W4"])
                OP("pool", lambda e: e.tensor_tensor(out=W[5][:, :], in0=W[4][:, :], in1=W[4][:, :], op=ALU.mult),
                   reads=["W4"], writes=["W5"])
                OP("pe", lambda e: e.matmul(post[0][0:64, :], lhsT=ones64, rhs=W[5][:, :], start=True, stop=True),
                   reads=["W5", "ones64"], writes=["post0"])
                OP("act", lambda e: e.activation(out=W[0][:, :], in_=post[0][0:64, :], func=AF.Sqrt, bias=eps_ap, scale=1.0),
                   reads=["post0", "epsb"], writes=["W0"])
                OP("dve", lambda e: e.reciprocal(out=W[1][:, :], in_=W[0][:, :]), reads=["W0"], writes=["W1"])
                OP("dve", lambda e: e.scalar_tensor_tensor(out=W[2][:, :], in0=W[4][:, :], scalar=gsub, in1=W[1][:, :],
                                                           op0=ALU.mult, op1=ALU.mult),
                   reads=["W4", "W1", "gsub"], writes=["W2"])
                finalize(W[2][:, :], ["W2"], 64 * h, q0)

        SCL = 0.125
        it = 0
        COMBOS = {1: [0, 1, 1, 1, 1, 1, 1, 2], 4: [0, 2] * 4, 16: [3] * 8}
        for h in range(6):
            for pi, dil in enumerate(DILS):
                lc, kcols, ntile = dil_geom(dil)
                tpc = lc // 128
                bi = it % 2
                it += 1
                kn, vn, qn = "KH%d" % bi, "VH%d" % bi, "QB%d" % bi
                OP("sync", lambda e, h=h, dil=dil, bi=bi, kcols=kcols: e.dma_start(out=KH[bi][0:64, 0:kcols], in_=lk[dil][h, :, :]),
                   writes=[kn], slot=kn)
                OP("sync", lambda e, h=h, dil=dil, bi=bi, ntile=ntile: e.dma_start(out=VH[bi][:, 0:ntile * VW], in_=lv[dil][h, :, :]),
                   writes=[vn], slot=vn)
                OP("sync", lambda e, h=h, dil=dil, bi=bi: e.dma_start(out=QB[bi][:, :], in_=lq[dil][h, :, :]), writes=[qn], slot=qn)
                vv = VH[bi][:, 0:ntile * VW].rearrange("p (t w) -> p t w", w=VW)
                for g in range(8):
                    s_ = sS[step % 2]
                    sn = "sS%d" % (step % 2)
                    p_ = PT[step % 2]
                    pn = "PT%d" % (step % 2)
                    a_ = acc[step % 2]
                    an = "acc%d" % (step % 2)
                    step += 1
                    for tl in range(4):
                        tau = g * 4 + tl
                        rho, j = divmod(tau, tpc)
                        kc0 = rho * (lc + 128) + 128 * j
                        for ab in range(2):
                            OP("pe", lambda e, tl=tl, ab=ab, kc0=kc0, tau=tau, s_=s_, bi=bi: e.matmul(
                                s_[:, tl * 256 + ab * 128: tl * 256 + ab * 128 + 128],
                                lhsT=KH[bi][0:64, kc0 + ab * 128: kc0 + ab * 128 + 128],
                                rhs=QB[bi][:, tau * 128:(tau + 1) * 128], start=True, stop=True),
                               reads=[kn, qn], writes=[sn])
                    OP("act", lambda e, s_=s_, p_=p_: e.activation(out=p_[:, :], in_=s_[:, :], func=AF.Exp, scale=SCL),
                       reads=[sn], writes=[pn])
                    cb = COMBOS[dil][g]
                    OP("dve", lambda e, p_=p_, cb=cb: e.tensor_tensor(out=p_[:, :], in0=p_[:, :], in1=mk[:, cb * 1024:(cb + 1) * 1024], op=ALU.mult),
                       reads=[pn, "mk"], writes=[pn])
                    for tl in range(4):
                        tau = g * 4 + tl
                        rho, j = divmod(tau, tpc)
                        vt0 = rho * (tpc + 1) + j
                        for ab in range(2):
                            OP("pe", lambda e, tl=tl, ab=ab, vt0=vt0, a_=a_, p_=p_, vv=vv: e.matmul(
                                a_[0:VW, tl * 128:(tl + 1) * 128], lhsT=vv[:, vt0 + ab, :],
                                rhs=p_[:, tl * 256 + ab * 128: tl * 256 + ab * 128 + 128], start=(ab == 0), stop=(ab == 1)),
                               reads=[vn, pn], writes=[an])
                    if dil == 1:
                        dst = accs[0:VW, g * 512:(g + 1) * 512]
                        src = a_[0:VW, :]
                    elif dil == 4:
                        rho, half = divmod(g, 2)
                        dst = accs[0:VW, :].rearrange("p (i r) -> p r i", r=4)[:, rho, half * 512:(half + 1) * 512]
                        src = a_[0:VW, :]
                    else:
                        dst = accs[0:VW, :].rearrange("p (i r) -> p r i", r=16)[:, 2 * g:2 * g + 2, :]
                        src = a_[0:VW, :].rearrange("p (r i) -> p r i", r=2)
                    if pi == 0:
                        OP("dve", lambda e, dst=dst, src=src: e.tensor_copy(out=dst, in_=src), reads=[an], writes=["accs"])
                    else:
                        OP("dve", lambda e, dst=dst, src=src: e.tensor_tensor(out=dst, in0=dst, in1=src, op=ALU.add),
                           reads=[an, "accs"], writes=["accs"])
            for qt in range(NBLK):
                src = accs[:, qt * 512:(qt + 1) * 512]
                swap_recip(src, ["accs"], qt % 2, qt % 2)
                OP("dve", lambda e, qt=qt, src=src: e.tensor_tensor(out=W[2 + qt % 2][:, :], in0=src[0:64, :], in1=W[qt % 2][:, :], op=ALU.mult),
                   reads=["accs", "W%d" % (qt % 2)], writes=["W%d" % (2 + qt % 2)])
                finalize(W[2 + qt % 2][:, :], ["W%d" % (2 + qt % 2)], 256 + 64 * h, qt * 512)

        for h in range(6):
            bi = it % 2
            it += 1
            kn, vn, qn = "KH%d" % bi, "VH%d" % bi, "QB%d" % bi
            OP("sync", lambda e, h=h, bi=bi: e.dma_start(out=KH[bi][0:64, 0:71 * 64], in_=nk[h, :, :]), writes=[kn], slot=kn)
            OP("sync", lambda e, h=h, bi=bi: e.dma_start(out=VH[bi][0:64, 0:71 * VW], in_=nv[h, :, :]), writes=[vn], slot=vn)
            OP("sync", lambda e, h=h, bi=bi: e.dma_start(out=QB[bi][:, :], in_=nq[h, :, :]), writes=[qn], slot=qn)
            OP("sync", lambda e, h=h: e.dma_start(out=E[:, :], in_=nbias[h, :, :]), writes=["E"], slot="E")
            OP("act", lambda e: e.activation(out=E[:, :], in_=E[:, :], func=AF.Exp), reads=["E"], writes=["E"])
            OP("dve", lambda e: e.tensor_tensor(out=E[:, :], in0=E[:, :], in1=nm[:, :], op=ALU.mult), reads=["E", "nm"], writes=["E"])
            vv = VH[bi][0:64, 0:71 * VW].rearrange("p (t w) -> p t w", w=VW)
            for rg in range(8):
                a_ = acc[rg % 2]
                an = "acc%d" % (rg % 2)
                for rr in range(8):
                    rl = rg * 8 + rr
                    var = rl + 1 if rl < 4 else (rl - 61 + 5 if rl >= 61 else 0)
                    s_ = sS[step % 2]
                    sn = "sS%d" % (step % 2)
                    p_ = PT[step % 2]
                    pn = "PT%d" % (step % 2)
                    x_ = XS[step % 2]
                    xn = "XS%d" % (step % 2)
                    step += 1
                    for j in range(8):
                        OP("pe", lambda e, j=j, rl=rl, s_=s_, bi=bi: e.matmul(
                            s_[0:64, j * 64:(j + 1) * 64], lhsT=KH[bi][0:64, (rl + j) * 64:(rl + j + 1) * 64],
                            rhs=QB[bi][:, rl * 64:(rl + 1) * 64], start=True, stop=True),
                           reads=[kn, qn], writes=[sn])
                    OP("act", lambda e, s_=s_, x_=x_: e.activation(out=x_[:, :], in_=s_[0:64, 0:512], func=AF.Exp, scale=SCL),
                       reads=[sn], writes=[xn])
                    OP("dve", lambda e, x_=x_, p_=p_, var=var: e.tensor_tensor(out=p_[0:64, 0:512], in0=x_[:, :], in1=E[:, var * 512:(var + 1) * 512], op=ALU.mult),
                       reads=[xn, "E"], writes=[pn])
                    for j in range(8):
                        OP("pe", lambda e, j=j, rl=rl, rr=rr, a_=a_, p_=p_, vv=vv: e.matmul(
                            a_[0:VW, rr * 64:(rr + 1) * 64], lhsT=vv[:, rl + j, :], rhs=p_[0:64, j * 64:(j + 1) * 64],
                            start=(j == 0), stop=(j == 7)),
                           reads=[vn, pn], writes=[an])
                ai = rg % 2
                OP("act", lambda e, a_=a_, ai=ai: e.activation(out=A01[ai][0:VW, :], in_=a_[0:VW, :], func=AF.Copy),
                   reads=[an], writes=["A%d" % ai])
                swap_recip(A01[ai], ["A%d" % ai], ai, ai)
                OP("dve", lambda e, ai=ai: e.tensor_tensor(out=W[2 + ai][:, :], in0=A01[ai][0:64, :], in1=W[ai][:, :], op=ALU.mult),
                   reads=["A%d" % ai, "W%d" % ai], writes=["W%d" % (2 + ai)])
                finalize(W[2 + ai][:, :], ["W%d" % (2 + ai)], 640 + 64 * h, rg * 512)
        sc.emit()
    return nc


def _bf(a):
    return np.ascontiguousarray(a).view(NPBF) if a.dtype == np.uint16 else np.ascontiguousarray(a)


def _vaug(vT, key_idx):
    valid = (key_idx >= 0) & (key_idx < S)
    idx = np.clip(key_idx, 0, S - 1)
    v = vT.T[idx]
    v = np.where(valid[..., None], v, np.zeros((), v.dtype))
    ones = np.ones(key_idx.shape + (VW - 64,), v.dtype)
    return np.concatenate([v, ones], axis=-1)


def _kcols(kT, key_idx):
    valid = (key_idx >= 0) & (key_idx < S)
    idx = np.clip(key_idx, 0, S - 1)
    k = kT[:, idx]
    return np.where(valid[None, :], k, np.zeros((), k.dtype))


def dil_masks(q):
    a = np.arange(128)[:, None]
    b = np.arange(128)[None, :]
    GA = (a >= b).astype(np.float32)
    GB = (a <= b).astype(np.float32)
    G = np.concatenate([GA, GB], 1)
    FA = GA.copy()
    LB = GB.copy()
    if q == 0:
        FA[0:64, :] = 0
    if q == 3:
        LB[64:128, :] = 0
    F = np.concatenate([FA, GB], 1)
    L = np.concatenate([GA, LB], 1)
    combos = [[F, G, G, G], [G, G, G, G], [G, G, G, L], [F, L, F, L]]
    return np.concatenate([np.concatenate(c, 1) for c in combos], 1).astype(NPBF)


def na_tables(rpb, q):
    kc = np.arange(64)[:, None]
    qc = np.arange(64)[None, :]
    dc = np.clip(kc - qc, -15, 15) + 15
    cs = np.clip(qc - 8, 0, 48)
    cmask = ((kc >= cs) & (kc < cs + 16)).astype(np.float32)
    dr = np.zeros((8, 8), np.int64)
    for var in range(8):
        for j in range(8):
            if var == 0:
                d = j + 3
            elif var <= 4:
                rl = var - 1
                if q == 0:
                    bsl = rl + j
                    g = bsl - 4 if bsl >= 4 else bsl + 4
                    d = g - rl + 7
                else:
                    d = j + 3
            else:
                rl = 61 + (var - 5)
                if q == 3:
                    bsl = rl + j
                    g = 188 + bsl if bsl <= 67 else 248 + (bsl - 68)
                    d = g - (192 + rl) + 7
                else:
                    d = j + 3
            dr[var, j] = d
    tab = rpb[:, dr[None, :, :, None], dc[:, None, None, :]]
    mask = np.broadcast_to(cmask[:, None, None, :], (64, 8, 8, 64))
    return np.ascontiguousarray(tab.reshape(6, 64, 8 * 512), np.float32), np.ascontiguousarray(mask.reshape(64, 8 * 512), np.float32)


def na_buffer_rows(q):
    rows = np.arange(71) + 64 * q - 4
    if q == 0:
        rows = np.array([4, 5, 6, 7] + list(range(0, 67)))
    if q == 3:
        rows = np.array(list(range(188, 256)) + [248, 249, 250])
    return rows


def prep_b(inp, l, qkv_full):
    lam_init = 0.8 - 0.6 * math.exp(-0.3 * l)
    selh = np.zeros((96, 64), np.float32)
    for i in range(64):
        selh[64 + i % 32, i] = 1.0
    maps = []
    for core in range(NCORE):
        b, q = divmod(core, 4)
        t0 = q * TQ
        full = qkv_full[b]
        qa, ka, va = full[0:256], full[256:512], full[512:768]
        qb, kb, vb = full[768:1152], full[1152:1536], full[1536:1920]
        qc, kc, vc = full[1920:2304], full[2304:2688], full[2688:3072]
        m = {}
        m["dq"] = np.ascontiguousarray(qa[:, t0:t0 + TQ].reshape(4, 64, TQ))
        m["dk"] = np.ascontiguousarray(ka.reshape(4, 64, S))
        keys = (np.arange(128)[None, :] * 128 + np.arange(128)[:, None])
        m["dv"] = np.stack([_vaug(va[64 * h:64 * h + 64], keys).reshape(128, 128 * VW) for h in range(4)])
        m["dlam"] = np.ascontiguousarray(np.broadcast_to(inp["diff_lambda"][l].reshape(1, 128), (64, 128)), np.float32)
        dcon = np.zeros((64, 4), np.float32)
        dcon[:, 0] = lam_init
        dcon[:, 1] = 1.0 - lam_init
        dcon[:, 2] = inp["diff_subln"][l]
        m["dcon"] = dcon
        m["selh"] = selh
        for dil in DILS:
            lc, kcols, ntile = dil_geom(dil)
            tpc = lc // 128
            i0 = t0 // dil
            loc = (np.arange(lc)[None, :] * dil + np.arange(dil)[:, None]).reshape(-1)
            m["lq%d" % dil] = np.ascontiguousarray(qb[:, t0 + loc].reshape(6, 64, TQ))
            kidx = ((i0 - 64 + np.arange(lc + 128))[None, :] * dil + np.arange(dil)[:, None])
            kidx = np.where((i0 - 64 + np.arange(lc + 128))[None, :] < 0, -1, kidx)
            kidx = np.where((i0 - 64 + np.arange(lc + 128))[None, :] >= S // dil, -1, kidx).reshape(-1)
            m["lk%d" % dil] = np.stack([_kcols(kb[64 * h:64 * h + 64], kidx) for h in range(6)])
            ci = i0 - 64 + 128 * np.arange(tpc + 1)[None, None, :] + np.arange(128)[:, None, None]
            tok = ci * dil + np.arange(dil)[None, :, None]
            tok = np.where((ci < 0) | (ci >= S // dil), -1, tok).reshape(128, ntile)
            m["lv%d" % dil] = np.stack([_vaug(vb[64 * h:64 * h + 64], tok).reshape(128, ntile * VW) for h in range(6)])
        m["lmask"] = dil_masks(q)
        rows = na_buffer_rows(q)
        tokn = (rows[:, None] * 64 + np.arange(64)[None, :])
        m["nq"] = np.ascontiguousarray(qc[:, t0:t0 + TQ].reshape(6, 64, TQ))
        m["nk"] = np.stack([np.ascontiguousarray(kc[64 * h:64 * h + 64][:, tokn.reshape(-1)]) for h in range(6)])
        m["nv"] = np.stack([_vaug(vc[64 * h:64 * h + 64], tokn.T).reshape(64, 71 * VW) for h in range(6)])
        tab, mask = na_tables(inp["na_rpb"][l], q)
        m["nbias"] = tab
        m["nmask"] = mask
        maps.append(m)
    return maps


NFC = DFF // 128


def build_stage_c(final):
    nc = new_nc()
    xh = nc.dram_tensor("xh", [D, TQ + 2], F32, kind="ExternalInput").ap()
    oh = nc.dram_tensor("oh", [D, TQ + 2], BF16, kind="ExternalInput").ap()
    edge = nc.dram_tensor("edge", [128, 2], F32, kind="ExternalInput").ap()
    cT = nc.dram_tensor("cT", [128, 8], F32, kind="ExternalInput").ap()
    wada = nc.dram_tensor("wada", [D, 4096], F32, kind="ExternalInput").ap()
    bada = nc.dram_tensor("bada", [128, 32], F32, kind="ExternalInput").ap()
    gff = nc.dram_tensor("gff", [128, 8], F32, kind="ExternalInput").ap()
    gfin = nc.dram_tensor("gfin", [128, 8], F32, kind="ExternalInput").ap()
    wout = nc.dram_tensor("wout", [D, D], F32, kind="ExternalInput").ap()
    wup = nc.dram_tensor("wup", [D, 2 * DFF], F32, kind="ExternalInput").ap()
    convw = nc.dram_tensor("convw", [128, NFC * 3], F32, kind="ExternalInput").ap()
    convb = nc.dram_tensor("convb", [128, NFC], F32, kind="ExternalInput").ap()
    wdown = nc.dram_tensor("wdown", [DFF, D], F32, kind="ExternalInput").ap()
    xo = nc.dram_tensor("xo", [D, TQ], F32, kind="ExternalOutput").ap()
    wupb = nc.dram_tensor("wupb", [NFC, 128, 8 * 256], BF16).ap()

    sc = Sched(nc)
    OP = sc.op
    with contextlib.ExitStack() as es:
        def sb(name, shape, dt):
            return es.enter_context(nc.sbuf_tensor(name, shape, dt))

        def ps(name, shape, dt=F32):
            return es.enter_context(nc.psum_tensor(name, shape, dt))

        NW = 514
        woutb = sb("woutb", [128, 8, D], BF16)
        wdnb = sb("wdnb", [128, NFC, D], BF16)
        XB = [sb("XB%d" % i, [128, 8, NW], F32) for i in range(2)]
        OBk = [sb("OBk%d" % i, [128, 8, NW], BF16) for i in range(2)]
        xm = sb("xm", [128, 8, NW], F32)
        sq = [sb("sq%d" % i, [128, NW], F32) for i in range(2)]
        rstd = sb("rstd", [128, NW], F32)
        ht = [sb("ht%d" % i, [128, NW], F32) for i in range(2)]
        h2 = sb("h2", [128, 8, NW], BF16)
        slab = [sb("slab%d" % i, [128, 8, 256], BF16) for i in range(3)]
        cv = [sb("cv%d" % i, [128, 512], F32) for i in range(3)]
        sl = [sb("sl%d" % i, [128, 512], F32) for i in range(2)]
        prod = sb("prod", [128, NFC, 512], BF16)
        small = sb("small", [128, 256], F32)
        ones = sb("ones", [128, 128], F32)
        T2 = [ps("T2%d" % i, [128, 1024]) for i in range(3)]
        V = [ps("V%d" % i, [128, 512]) for i in range(2)]

        c_ap, silu_ap, mod_ap, bada_ap = small[:, 0:8], small[:, 8:16], small[:, 16:48], small[:, 48:80]
        gta, shf, scf, gtf = small[:, 16:24], small[:, 24:32], small[:, 32:40], small[:, 40:48]
        gff_ap, gscf, gfin_ap = small[:, 80:88], small[:, 88:96], small[:, 96:104]
        cw_ap, cb_ap, edge_ap, eps_ap = small[:, 104:170], small[:, 170:192], small[:, 192:194], small[:, 194:195]
        for nm_, ap_, src_ in (("cT", c_ap, cT), ("bada", bada_ap, bada), ("gff", gff_ap, gff), ("gfin", gfin_ap, gfin),
                               ("convw", cw_ap, convw), ("convb", cb_ap, convb), ("edge", edge_ap, edge)):
            OP("sync", lambda e, ap_=ap_, src_=src_: e.dma_start(out=ap_, in_=src_), writes=[nm_], slot=nm_)
        OP("pool", lambda e: e.memset(ones[:, :], 1.0 / D), writes=["ones"])
        OP("pool", lambda e: e.memset(eps_ap, EPS), writes=["epsc"])
        OP("act", lambda e: e.activation(out=silu_ap, in_=c_ap, func=AF.Silu), reads=["cT"], writes=["silu"])

        xparts = [["XB%dk%d" % (i, k) for k in range(8)] for i in range(2)]
        for jq in range(8):
            wa = XB[jq % 2]
            for kc in range(8):
                OP("sync" if kc % 2 == 0 else "poolq",
                   lambda e, kc=kc, wa=wa, jq=jq: e.dma_start(out=wa[:, kc, 0:512], in_=wada[kc * 128:(kc + 1) * 128, jq * 512:(jq + 1) * 512]),
                   writes=[xparts[jq % 2][kc]], slot="XB%d" % (jq % 2))
            for jj in range(4):
                j = jq * 4 + jj
                for kc in range(8):
                    OP("pe", lambda e, j=j, jj=jj, kc=kc, wa=wa: e.matmul(T2[0][:, j:j + 1], lhsT=wa[:, kc, jj * 128:(jj + 1) * 128],
                                                                         rhs=silu_ap[:, kc:kc + 1], start=(kc == 0), stop=(kc == 7)),
                       reads=xparts[jq % 2] + ["silu"], writes=["T20"])
        OP("dve", lambda e: e.tensor_tensor(out=mod_ap, in0=T2[0][:, 0:32], in1=bada_ap, op=ALU.add), reads=["T20", "bada"], writes=["mod"])
        OP("dve", lambda e: e.scalar_tensor_tensor(out=gscf, in0=scf, scalar=1.0, in1=gff_ap, op0=ALU.add, op1=ALU.mult),
           reads=["mod", "gff"], writes=["gscf"])

        stg = xm[:, :, :].rearrange("p a b -> p (a b)")
        for kc in range(8):
            OP("sync", lambda e, kc=kc: e.dma_start(out=stg[:, 0:D], in_=wout[kc * 128:(kc + 1) * 128, :]), writes=["stgA"], slot="stgA")
            OP("act", lambda e, kc=kc: e.activation(out=woutb[:, kc, :], in_=stg[:, 0:D], func=AF.Copy), reads=["stgA"], writes=["woutb"])
        for fc in range(NFC):
            half = "stgA" if fc % 2 == 0 else "stgB"
            off = 0 if fc % 2 == 0 else 1024
            OP("sync", lambda e, fc=fc, off=off: e.dma_start(out=stg[:, off:off + D], in_=wdown[fc * 128:(fc + 1) * 128, :]), writes=[half], slot=half)
            OP("act" if fc % 2 == 0 else "pool", lambda e, fc=fc, off=off: (e.activation(out=wdnb[:, fc, :], in_=stg[:, off:off + D], func=AF.Copy)
                                                                           if fc % 2 == 0 else e.tensor_copy(out=wdnb[:, fc, :], in_=stg[:, off:off + D])),
               reads=[half], writes=["wdnb"])
        wupv = wupb.rearrange("f p (k g c) -> p k g f c", k=8, g=2)
        prodf = prod[:, :, :].rearrange("p a b -> p (a b)")
        for kc in range(8):
            base = (kc % 2) * 2 * DFF
            stn = "stb%d" % (kc % 2)
            for g in range(2):
                xbuf = XB[g][:, :, :].rearrange("p a b -> p (a b)")
                OP("sync" if g == 0 else "poolq",
                   lambda e, kc=kc, g=g, xbuf=xbuf: e.dma_start(out=xbuf[:, 0:DFF], in_=wup[kc * 128:(kc + 1) * 128, g * DFF:(g + 1) * DFF]),
                   writes=xparts[g], slot="XB%d" % g)
                if g == 0:
                    OP("act", lambda e, xbuf=xbuf, base=base: e.activation(out=prodf[:, base:base + DFF], in_=xbuf[:, 0:DFF], func=AF.Copy),
                       reads=xparts[g], writes=[stn + "g"])
                else:
                    OP("pool", lambda e, xbuf=xbuf, base=base: e.tensor_copy(out=prodf[:, base + DFF:base + 2 * DFF], in_=xbuf[:, 0:DFF]),
                       reads=xparts[g], writes=[stn + "u"])
            for g in range(2):
                src = prodf[:, base + g * DFF:base + (g + 1) * DFF].rearrange("p (f c) -> p f c", f=NFC)
                OP("sync", lambda e, kc=kc, g=g, src=src: e.dma_start(out=wupv[:, kc, g], in_=src),
                   reads=[stn + "gu"[g]], writes=["wupb%d_%d" % (kc, g)], slot=stn + "gu"[g])
        wupn = ["wupb%d_%d" % (k, g) for k in range(8) for g in range(2)]

        slab_i = [0]
        slab_q = []

        def issue_slab(fc):
            i = slab_i[0] % 3
            slab_i[0] += 1
            OP("sync", lambda e, fc=fc, i=i: e.dma_start(out=slab[i][:, :, :].rearrange("p a b -> p (a b)"), in_=wupb[fc, :, :]),
               reads=wupn, writes=["slab%d" % i], slot="slab%d" % i)
            slab_q.append(i)

        def load_block(blk):
            x_ = XB[blk % 2]
            o_ = OBk[blk % 2]
            c0 = blk * 512
            for kc in range(8):
                OP("sync" if kc % 2 == 0 else "poolq",
                   lambda e, kc=kc, x_=x_, c0=c0: e.dma_start(out=x_[:, kc, :], in_=xh[kc * 128:(kc + 1) * 128, c0:c0 + NW]),
                   writes=[xparts[blk % 2][kc]], slot="XB%d" % (blk % 2))
            OP("poolq", lambda e, o_=o_, c0=c0: e.dma_start(out=o_[:, :, :], in_=oh.rearrange("(k p) t -> p k t", p=128)[:, :, c0:c0 + NW]),
               writes=["OBk%d" % (blk % 2)], slot="OBk%d" % (blk % 2))

        load_block(0)
        issue_slab(0)
        issue_slab(1)
        for blk in range(NBLK):
            x_ = XB[blk % 2]
            o_ = OBk[blk % 2]
            xp = xparts[blk % 2]
            on_ = "OBk%d" % (blk % 2)
            if blk + 1 < NBLK:
                load_block(blk + 1)
            for oc in range(8):
                t2 = T2[oc % 2]
                tn = "T2%d" % (oc % 2)
                for (c_lo, c_hi) in ((0, 512), (512, NW)):
                    for kc in range(8):
                        OP("pe", lambda e, oc=oc, kc=kc, t2=t2, o_=o_, c_lo=c_lo, c_hi=c_hi: e.matmul(
                            t2[:, c_lo:c_hi], lhsT=woutb[:, kc, oc * 128:(oc + 1) * 128], rhs=o_[:, kc, c_lo:c_hi],
                            start=(kc == 0), stop=(kc == 7)),
                           reads=[on_, "woutb"], writes=[tn])
                OP("dve", lambda e, oc=oc, t2=t2, x_=x_: e.scalar_tensor_tensor(out=xm[:, oc, :], in0=t2[:, 0:NW], scalar=gta[:, oc:oc + 1],
                                                                              in1=x_[:, oc, :], op0=ALU.mult, op1=ALU.add),
                   reads=[tn, "mod", "wdnb"] + xp, writes=["xm%d" % oc])
                s_ = sq[oc % 2]
                sn = "sq%d" % (oc % 2)
                OP("act", lambda e, oc=oc, s_=s_: e.activation(out=s_[:, :], in_=xm[:, oc, :], func=AF.Square), reads=["xm%d" % oc], writes=[sn])
                for (c_lo, c_hi) in ((0, 512), (512, NW)):
                    OP("pe", lambda e, oc=oc, s_=s_, c_lo=c_lo, c_hi=c_hi: e.matmul(T2[2][:, c_lo:c_hi], lhsT=ones[:, :], rhs=s_[:, c_lo:c_hi],
                                                                                  start=(oc == 0), stop=(oc == 7)),
                       reads=[sn, "ones"], writes=["T22" if c_lo == 0 else "T22b"])
            xmn = ["xm%d" % k for k in range(8)]
            OP("act", lambda e: e.activation(out=rstd[:, :], in_=T2[2][:, 0:NW], func=AF.Sqrt, bias=eps_ap, scale=1.0),
               reads=["T22", "T22b", "epsc"], writes=["rstd0"])
            OP("dve", lambda e: e.reciprocal(out=rstd[:, :], in_=rstd[:, :]), reads=["rstd0"], writes=["rstd"])
            for kc in range(8):
                h_ = ht[kc % 2]
                hn = "ht%d" % (kc % 2)
                OP("dve", lambda e, kc=kc, h_=h_: e.scalar_tensor_tensor(out=h_[:, :], in0=xm[:, kc, :], scalar=gscf[:, kc:kc + 1], in1=rstd[:, :],
                                                                        op0=ALU.mult, op1=ALU.mult),
                   reads=["xm%d" % kc, "rstd", "gscf"], writes=[hn])
                OP("pool", lambda e, kc=kc, h_=h_: e.tensor_scalar(out=h2[:, kc, :], in0=h_[:, :], scalar1=shf[:, kc:kc + 1], scalar2=None, op0=ALU.add),
                   reads=[hn, "mod"], writes=["h2_%d" % kc])
            h2n = ["h2_%d" % k for k in range(8)]
            if blk == 0:
                OP("pool", lambda e: e.tensor_scalar(out=h2[:, :, 0:1], in0=h2[:, :, 0:1], scalar1=edge_ap[:, 0:1], scalar2=None, op0=ALU.mult),
                   reads=h2n + ["edge"], writes=h2n)
            if blk == NBLK - 1:
                OP("pool", lambda e: e.tensor_scalar(out=h2[:, :, NW - 1:NW], in0=h2[:, :, NW - 1:NW], scalar1=edge_ap[:, 1:2], scalar2=None, op0=ALU.mult),
                   reads=h2n + ["edge"], writes=h2n)
            for fc in range(NFC):
                nxt = blk * NFC + fc + 2
                if nxt < NBLK * NFC:
                    issue_slab(nxt % NFC)
                si = slab_q.pop(0)
                sl_ = slab[si]
                sln = "slab%d" % si
                pg = T2[fc % 2]
                pgn = "T2%d" % (fc % 2)
                pu = V[fc % 2]
                pun = "V%d" % (fc % 2)
                for (c_lo, c_hi) in ((0, 512), (512, NW)):
                    for kc in range(8):
                        OP("pe", lambda e, kc=kc, pg=pg, sl_=sl_, c_lo=c_lo, c_hi=c_hi: e.matmul(
                            pg[:, c_lo:c_hi], lhsT=sl_[:, kc, 0:128], rhs=h2[:, kc, c_lo:c_hi], start=(kc == 0), stop=(kc == 7)),
                           reads=h2n + [sln], writes=[pgn])
                for kc in range(8):
                    OP("pe", lambda e, kc=kc, pu=pu, sl_=sl_: e.matmul(pu[:, :], lhsT=sl_[:, kc, 128:256], rhs=h2[:, kc, 1:513],
                                                                       start=(kc == 0), stop=(kc == 7)),
                       reads=h2n + [sln], writes=[pun])
                c_ = cv[fc % 3]
                cn = "cv%d" % (fc % 3)
                OP("dve", lambda e, fc=fc, pg=pg, c_=c_: e.tensor_scalar(out=c_[:, :], in0=pg[:, 0:512], scalar1=cw_ap[:, fc * 3:fc * 3 + 1], scalar2=None, op0=ALU.mult),
                   reads=[pgn, "convw"], writes=[cn])
                OP("dve", lambda e, fc=fc, pg=pg, c_=c_: e.scalar_tensor_tensor(out=c_[:, :], in0=pg[:, 1:513], scalar=cw_ap[:, fc * 3 + 1:fc * 3 + 2], in1=c_[:, :],
                                                                               op0=ALU.mult, op1=ALU.add),
                   reads=[pgn, "convw", cn], writes=[cn])
                OP("dve", lambda e, fc=fc, pg=pg, c_=c_: e.scalar_tensor_tensor(out=c_[:, :], in0=pg[:, 2:514], scalar=cw_ap[:, fc * 3 + 2:fc * 3 + 3], in1=c_[:, :],
                                                                               op0=ALU.mult, op1=ALU.add),
                   reads=[pgn, "convw", cn], writes=[cn])
                s2 = sl[fc % 2]
                s2n = "sl%d" % (fc % 2)
                OP("act", lambda e, fc=fc, c_=c_, s2=s2: e.activation(out=s2[:, :], in_=c_[:, :], func=AF.Silu, bias=cb_ap[:, fc:fc + 1], scale=1.0),
                   reads=[cn, "convb"], writes=[s2n])
                OP("dve", lambda e, fc=fc, s2=s2, pu=pu: e.tensor_tensor(out=prod[:, fc, :], in0=s2[:, :], in1=pu[:, :], op=ALU.mult),
                   reads=[s2n, pun], writes=["prod%d" % fc])
            prn = ["prod%d" % f for f in range(NFC)]
            for oc in range(8):
                pd = T2[2][:, (oc % 2) * 512:(oc % 2) * 512 + 512]
                pdn = "T22" if oc % 2 == 0 else "T22b"
                for fc in range(NFC):
                    OP("pe", lambda e, oc=oc, fc=fc, pd=pd: e.matmul(pd, lhsT=wdnb[:, fc, oc * 128:(oc + 1) * 128], rhs=prod[:, fc, :],
                                                                    start=(fc == 0), stop=(fc == NFC - 1)),
                       reads=prn + ["wdnb"], writes=[pdn])
                OP("dve", lambda e, oc=oc, pd=pd: e.scalar_tensor_tensor(out=xm[:, oc, 1:513], in0=pd, scalar=gtf[:, oc:oc + 1], in1=xm[:, oc, 1:513],
                                                                        op0=ALU.mult, op1=ALU.add),
                   reads=[pdn, "mod", "xm%d" % oc], writes=["xm%d" % oc])
            if final:
                for oc in range(8):
                    s_ = sq[oc % 2]
                    sn = "sq%d" % (oc % 2)
                    OP("act", lambda e, oc=oc, s_=s_: e.activation(out=s_[:, 0:512], in_=xm[:, oc, 1:513], func=AF.Square), reads=["xm%d" % oc], writes=[sn])
                    OP("pe", lambda e, oc=oc, s_=s_: e.matmul(T2[0][:, 0:512], lhsT=ones[:, :], rhs=s_[:, 0:512], start=(oc == 0), stop=(oc == 7)),
                       reads=[sn, "ones"], writes=["T20"])
                OP("act", lambda e: e.activation(out=rstd[:, 0:512], in_=T2[0][:, 0:512], func=AF.Sqrt, bias=eps_ap, scale=1.0),
                   reads=["T20", "epsc"], writes=["rstd0"])
                OP("dve", lambda e: e.reciprocal(out=rstd[:, 0:512], in_=rstd[:, 0:512]), reads=["rstd0"], writes=["rstd"])
                for oc in range(8):
                    OP("dve", lambda e, oc=oc: e.scalar_tensor_tensor(out=xm[:, oc, 1:513], in0=xm[:, oc, 1:513], scalar=gfin_ap[:, oc:oc + 1], in1=rstd[:, 0:512],
                                                                     op0=ALU.mult, op1=ALU.mult),
                       reads=["xm%d" % oc, "rstd", "gfin"], writes=["xm%d" % oc])
            for oc in range(8):
                OP("sync" if oc % 2 == 0 else "poolq",
                   lambda e, oc=oc, blk=blk: e.dma_start(out=xo[oc * 128:(oc + 1) * 128, blk * 512:(blk + 1) * 512], in_=xm[:, oc, 1:513]),
                   reads=["xm%d" % oc], writes=["xo"], slot="xm%d" % oc)
        sc.emit()
    return nc


def prep_c(inp, l, xT_cores, oT_cores):
    maps = []
    for core in range(NCORE):
        b, q = divmod(core, 4)

        def halo(arrs, dt):
            out = np.zeros((D, TQ + 2), dt)
            out[:, 1:TQ + 1] = arrs[core]
            if q > 0:
                out[:, 0] = arrs[core - 1][:, TQ - 1]
            if q < 3:
                out[:, TQ + 1] = arrs[core + 1][:, 0]
            return out

        edge = np.zeros((128, 2), np.float32)
        edge[:, 0] = 1.0 if q > 0 else 0.0
        edge[:, 1] = 1.0 if q < 3 else 0.0
        cw = np.ascontiguousarray(inp["conv_w"][l].reshape(3, NFC, 128).transpose(2, 1, 0).reshape(128, NFC * 3), np.float32)
        m = {
            "xh": halo(xT_cores, np.float32),
            "oh": halo(oT_cores, NPBF),
            "edge": edge,
            "cT": _pc(inp["c"][b], 8),
            "wada": np.ascontiguousarray(inp["w_ada"][l][:, 2048:6144]),
            "bada": _pc(inp["b_ada"][l][2048:6144], 32),
            "gff": _pc(inp["g_ffn"][l], 8),
            "gfin": _pc(inp["g_final"], 8),
            "wout": np.ascontiguousarray(inp["w_out"][l]),
            "wup": np.ascontiguousarray(inp["w_up"][l]),
            "convw": cw,
            "convb": _pc(inp["conv_b"][l], NFC),
            "wdown": np.ascontiguousarray(inp["w_down"][l]),
        }
        maps.append(m)
    return maps


_NC_CACHE = {}


def _get_nc(name):
    if name not in _NC_CACHE:
        if name == "a":
            _NC_CACHE[name] = build_stage_a()
        elif name == "b":
            _NC_CACHE[name] = build_stage_b()
        elif name == "c0":
            _NC_CACHE[name] = build_stage_c(False)
        else:
            _NC_CACHE[name] = build_stage_c(True)
    return _NC_CACHE[name]


def _run(name, maps):
    if name == "a":
        nc = build_stage_a()
    elif name == "b":
        nc = build_stage_b()
    else:
        nc = build_stage_c(name == "c1")
    return run_bass_kernel_spmd(nc, maps, core_ids=list(range(NCORE))).results


def kernel(**inp):
    inp = {k: np.asarray(v) for k, v in inp.items()}
    x = inp["x"]
    xT = [np.ascontiguousarray(x[c // 4, (c % 4) * TQ:(c % 4 + 1) * TQ, :].T) for c in range(NCORE)]
    for l in range(DEPTH):
        ra = _run("a", prep_a(inp, l, xT))
        qkv = [np.asarray(r["qkvT"]) for r in ra]
        full = [np.concatenate([qkv[b * 4 + q] for q in range(4)], axis=1) for b in range(B)]
        rb = _run("b", prep_b(inp, l, full))
        oT = [np.asarray(r["oT"]) for r in rb]
        rc = _run("c1" if l == DEPTH - 1 else "c0", prep_c(inp, l, xT, oT))
        xT = [np.asarray(r["xo"]) for r in rc]
    out = np.empty((B, S, D), np.float32)
    for c in range(NCORE):
        out[c // 4, (c % 4) * TQ:(c % 4 + 1) * TQ, :] = xT[c].T
    return out
```

```python
import contextlib
import math
import numpy as np
import ml_dtypes
import concourse.bass as bass
import concourse.mybir as mybir
from concourse.bass_utils import run_bass_kernel_spmd

F32 = mybir.dt.float32
BF16 = mybir.dt.bfloat16
ALU = mybir.AluOpType
AF = mybir.ActivationFunctionType
NPBF = ml_dtypes.bfloat16

D = 1024
S = 16384
B = 2
DEPTH = 2
NCORE = 8
TQ = 4096
NBLK = TQ // 512
DFF = 2816
EPS = 1e-6


class Sched:
    COMPUTE = ("pe", "act", "dve", "pool")

    def __init__(self, nc):
        self.nc = nc
        self.ops = []
        self.res = {}
        self.slot_cnt = {}

    def op(self, eng, fn, reads=(), writes=(), slot=None):
        o = dict(eng=eng, fn=fn, reads=tuple(reads), writes=tuple(writes), slot=slot,
                 deps=[], signal=False, idx=len(self.ops))
        stream = {"poolq": "pool", "actq": "act"}.get(eng, eng)
        o["stream"] = stream
        o["is_dma"] = eng in ("sync", "poolq", "actq")
        deps = []
        for r in o["reads"]:
            st = self.res.setdefault(r, dict(w=[], r=[]))
            deps += st["w"]
        for w in o["writes"]:
            st = self.res.setdefault(w, dict(w=[], r=[]))
            deps += st["w"] + st["r"]
        seen = set()
        for d in deps:
            if d["idx"] in seen:
                continue
            seen.add(d["idx"])
            if (not d["is_dma"]) and d["stream"] == stream and stream == "pe" and not o["is_dma"]:
                continue
            o["deps"].append(d)
            d["signal"] = True
        for r in o["reads"]:
            self.res[r]["r"].append(o)
        for w in o["writes"]:
            self.res[w] = dict(w=[o], r=[])
        if o["is_dma"]:
            assert slot is not None
            o["signal"] = True
        self.ops.append(o)
        return o

    def emit(self, final_wait_stream="sync"):
        nc = self.nc
        sem_names = []
        counters = {}
        for o in self.ops:
            if not o["signal"]:
                continue
            key = ("dma", o["slot"], o["eng"]) if o["is_dma"] else ("eng", o["stream"])
            if key not in counters:
                counters[key] = 0
                sem_names.append(key)
            counters[key] += 16 if o["is_dma"] else 1
            o["ev"] = (key, counters[key])
        final = {}
        for o in self.ops:
            if o["signal"]:
                final[o["ev"][0]] = max(final.get(o["ev"][0], 0), o["ev"][1])
        with contextlib.ExitStack() as es:
            sems = {}
            for i, key in enumerate(sem_names):
                sems[key] = es.enter_context(nc.semaphore("s%d" % i))
            block = es.enter_context(nc.Block())
            streams = {}
            for o in self.ops:
                streams.setdefault(o["stream"], []).append(o)

            def run_stream(name, engobj):
                known = {}
                for o in streams.get(name, []):
                    need = {}
                    for d in o["deps"]:
                        key, val = d["ev"]
                        need[key] = max(need.get(key, 0), val)
                    for key, val in need.items():
                        if known.get(key, 0) < val:
                            engobj.wait_ge(sems[key], val)
                            known[key] = val
                    inst = o["fn"](engobj)
                    if o["signal"]:
                        key, val = o["ev"]
                        inst.then_inc(sems[key], 16 if o["is_dma"] else 1)
                        if not o["is_dma"] and False:
                            known[key] = val
                if name == final_wait_stream:
                    for key, val in final.items():
                        if known.get(key, 0) < val:
                            engobj.wait_ge(sems[key], val)

            @block.sync
            def _(e):
                run_stream("sync", e)

            @block.tensor
            def _(e):
                run_stream("pe", e)

            @block.scalar
            def _(e):
                run_stream("act", e)

            @block.vector
            def _(e):
                run_stream("dve", e)

            @block.gpsimd
            def _(e):
                run_stream("pool", e)


def new_nc():
    return bass.Bass("TRN2", target_bir_lowering=False)


ROPE_CHUNKS = [0, 1, 2, 3, 6, 7, 8, 9, 10, 11]


def build_stage_a():
    nc = new_nc()
    xT = nc.dram_tensor("xT", [D, TQ], F32, kind="ExternalInput").ap()
    cT = nc.dram_tensor("cT", [128, 8], F32, kind="ExternalInput").ap()
    wada = nc.dram_tensor("wada", [D, 2048], F32, kind="ExternalInput").ap()
    bada = nc.dram_tensor("bada", [128, 16], F32, kind="ExternalInput").ap()
    gat = nc.dram_tensor("gat", [128, 8], F32, kind="ExternalInput").ap()
    win = nc.dram_tensor("win", [D, 3072], F32, kind="ExternalInput").ap()
    cosd = nc.dram_tensor("cosd", [128, TQ], F32, kind="ExternalInput").ap()
    sind = nc.dram_tensor("sind", [128, TQ], F32, kind="ExternalInput").ap()
    cosl = nc.dram_tensor("cosl", [128, TQ], F32, kind="ExternalInput").ap()
    sinl = nc.dram_tensor("sinl", [128, TQ], F32, kind="ExternalInput").ap()
    qkvT = nc.dram_tensor("qkvT", [3072, TQ], BF16, kind="ExternalOutput").ap()

    sc = Sched(nc)
    with contextlib.ExitStack() as es:
        def sb(name, shape, dt):
            return es.enter_context(nc.sbuf_tensor(name, shape, dt))

        def ps(name, shape, dt=F32):
            return es.enter_context(nc.psum_tensor(name, shape, dt))

        wbf = sb("wbf", [128, 8, 3072], BF16)
        wrot = sb("wrot", [128, 8, 1280], BF16)
        wtmp = [sb("wtmp%d" % i, [128, 1536], F32) for i in range(2)]
        small = sb("small", [128, 64], F32)
        ones = sb("ones", [128, 128], F32)
        xb = [sb("xb%d" % i, [128, 8, 512], F32) for i in range(2)]
        xsq = sb("xsq", [128, 8, 512], F32)
        rstd = sb("rstd", [128, 512], F32)
        htmp = [sb("htmp%d" % i, [128, 512], F32) for i in range(2)]
        hT = sb("hT", [128, 8, 512], BF16)
        tabs = [sb("tabs%d" % i, [128, 4, 512], F32) for i in range(2)]
        r1 = [sb("r1_%d" % i, [128, 512], F32) for i in range(2)]
        r2 = [sb("r2_%d" % i, [128, 512], F32) for i in range(2)]
        ob = [sb("ob%d" % i, [128, 512], BF16) for i in range(4)]
        p_mod = ps("p_mod", [128, 16])
        p_ms = ps("p_ms", [128, 512])
        p_q = [ps("p_q%d" % i, [128, 512]) for i in range(2)]
        p_r = [ps("p_r%d" % i, [128, 512]) for i in range(2)]

        c_ap = small[:, 0:8]
        silu_ap = small[:, 8:16]
        mod_ap = small[:, 16:32]
        bada_ap = small[:, 32:48]
        gat_ap = small[:, 48:56]
        gsc_ap = small[:, 56:64]
        sh_ap = small[:, 16:24]

        sc.op("sync", lambda e: e.dma_start(out=c_ap, in_=cT), writes=["cT"], slot="cT")
        sc.op("sync", lambda e: e.dma_start(out=bada_ap, in_=bada), writes=["bada"], slot="bada")
        sc.op("sync", lambda e: e.dma_start(out=gat_ap, in_=gat), writes=["gat"], slot="gat")
        sc.op("pool", lambda e: e.memset(ones[:, :], 1.0 / D), writes=["ones"])
        epsc = sb("epsc", [128, 1], F32)
        eps_ap = epsc[:, 0:1]
        sc.op("pool", lambda e: e.memset(epsc[:, :], EPS), writes=["epsc"])
        sc.op("act", lambda e: e.activation(out=silu_ap, in_=c_ap, func=AF.Silu), reads=["cT"], writes=["silu"])

        xparts = [["xb%dk%d" % (i, k) for k in range(8)] for i in range(2)]
        for jq in range(4):
            wa = xb[jq % 2]
            for kc in range(8):
                sc.op("sync" if kc % 2 == 0 else "poolq",
                      lambda e, kc=kc, wa=wa, jq=jq: e.dma_start(out=wa[:, kc, :], in_=wada[kc * 128:(kc + 1) * 128, jq * 512:(jq + 1) * 512]),
                      writes=[xparts[jq % 2][kc]], slot="xb%d" % (jq % 2))
            for jj in range(4):
                j = jq * 4 + jj
                for kc in range(8):
                    sc.op("pe", lambda e, j=j, jj=jj, kc=kc, wa=wa: e.matmul(p_mod[:, j:j + 1], lhsT=wa[:, kc, jj * 128:(jj + 1) * 128],
                                                                            rhs=silu_ap[:, kc:kc + 1], start=(kc == 0), stop=(kc == 7)),
                          reads=xparts[jq % 2] + ["silu"], writes=["p_mod"])
        sc.op("dve", lambda e: e.tensor_tensor(out=mod_ap, in0=p_mod[:, :], in1=bada_ap, op=ALU.add),
              reads=["p_mod", "bada"], writes=["mod"])
        sc.op("dve", lambda e: e.scalar_tensor_tensor(out=gsc_ap, in0=small[:, 24:32], scalar=1.0, in1=gat_ap,
                                                      op0=ALU.add, op1=ALU.mult),
              reads=["mod", "gat"], writes=["gsc"])

        for kc in range(8):
            for hf in range(2):
                t = wtmp[hf]
                tnm = "wtmp%d" % hf
                c_lo = hf * 1536
                sc.op("sync" if hf == 0 else "poolq",
                      lambda e, kc=kc, t=t, c_lo=c_lo: e.dma_start(out=t[:, :], in_=win[kc * 128:(kc + 1) * 128, c_lo:c_lo + 1536]),
                      writes=[tnm], slot=tnm)
                sc.op("act", lambda e, kc=kc, t=t, c_lo=c_lo: e.activation(out=wbf[:, kc, c_lo:c_lo + 1536], in_=t[:, :], func=AF.Copy),
                      reads=[tnm], writes=["wbf%d_%d" % (kc, hf)])
                if hf == 0:
                    for (c0, n, half, r0) in ((0, 512, 16, 0), (768, 768, 32, 512)):
                        src = t[:, c0:c0 + n].rearrange("p (g h d) -> p g h d", h=2, d=half)
                        dst = wrot[:, kc, r0:r0 + n].rearrange("p (g h d) -> p g h d", h=2, d=half)
                        sc.op("dve", lambda e, src=src, dst=dst: e.tensor_scalar(out=dst[:, :, 0, :], in0=src[:, :, 1, :],
                                                                                 scalar1=-1.0, scalar2=None, op0=ALU.mult),
                              reads=[tnm], writes=["wrot%d_%da" % (kc, c0)])
                        sc.op("dve", lambda e, src=src, dst=dst: e.tensor_copy(out=dst[:, :, 1, :], in_=src[:, :, 0, :]),
                              reads=[tnm], writes=["wrot%d_%db" % (kc, c0)])

        wnames = (["wbf%d_%d" % (k, h) for k in range(8) for h in range(2)]
                  + ["wrot%d_%d%s" % (k, c0, ab) for k in range(8) for c0 in (0, 768) for ab in "ab"])
        obi = 0
        for blk in range(NBLK):
            t0 = blk * 512
            x_ = xb[blk % 2]
            tb = tabs[blk % 2]
            xn = "xb%d" % (blk % 2)
            tn = "tabs%d" % (blk % 2)
            xp = xparts[blk % 2]
            tp = ["%sk%d" % (tn, i) for i in range(4)]
            for kc in range(8):
                sc.op("sync" if kc % 2 == 0 else "poolq",
                      lambda e, kc=kc, x_=x_, t0=t0: e.dma_start(out=x_[:, kc, :], in_=xT[kc * 128:(kc + 1) * 128, t0:t0 + 512]),
                      writes=[xp[kc]], slot=xn)
            for i, tab in enumerate((cosd, sind, cosl, sinl)):
                sc.op("poolq", lambda e, i=i, tab=tab, tb=tb, t0=t0: e.dma_start(out=tb[:, i, :], in_=tab[:, t0:t0 + 512]),
                      writes=[tp[i]], slot=tn)
            sc.op("act", lambda e, x_=x_: e.activation(out=xsq[:, :, :], in_=x_[:, :, :], func=AF.Square),
                  reads=xp, writes=["xsq"])
            for kc in range(8):
                sc.op("pe", lambda e, kc=kc: e.matmul(p_ms[:, :], lhsT=ones[:, :], rhs=xsq[:, kc, :], start=(kc == 0), stop=(kc == 7)),
                      reads=["xsq", "ones"], writes=["p_ms"])
            sc.op("act", lambda e: e.activation(out=rstd[:, :], in_=p_ms[:, :], func=AF.Sqrt, bias=eps_ap, scale=1.0),
                  reads=["p_ms", "epsc"], writes=["rstd0"])
            sc.op("dve", lambda e: e.reciprocal(out=rstd[:, :], in_=rstd[:, :]),
                  reads=["rstd0"], writes=["rstd"])
            for kc in range(8):
                ht = htmp[kc % 2]
                hn = "htmp%d" % (kc % 2)
                sc.op("dve", lambda e, kc=kc, ht=ht, x_=x_: e.tensor_tensor(out=ht[:, :], in0=x_[:, kc, :], in1=rstd[:, :], op=ALU.mult),
                      reads=xp + ["rstd"], writes=[hn])
                sc.op("act", lambda e, kc=kc, ht=ht: e.activation(out=hT[:, kc, :], in_=ht[:, :], func=AF.Identity,
                                                                  bias=sh_ap[:, kc:kc + 1], scale=gsc_ap[:, kc:kc + 1]),
                      reads=[hn, "mod", "gsc"], writes=["hT%d" % kc])
            hnames = ["hT%d" % k for k in range(8)]
            for oc in range(24):
                pq = p_q[oc % 2]
                pqn = "p_q%d" % (oc % 2)
                for kc in range(8):
                    sc.op("pe", lambda e, kc=kc, oc=oc, pq=pq: e.matmul(pq[:, :], lhsT=wbf[:, kc, oc * 128:(oc + 1) * 128], rhs=hT[:, kc, :],
                                                                        start=(kc == 0), stop=(kc == 7)),
                          reads=hnames + wnames, writes=[pqn])
                o_ = ob[obi % 4]
                on = "ob%d" % (obi % 4)
                obi += 1
                if oc in ROPE_CHUNKS:
                    ri = ROPE_CHUNKS.index(oc)
                    pr = p_r[ri % 2]
                    prn = "p_r%d" % (ri % 2)
                    for kc in range(8):
                        sc.op("pe", lambda e, kc=kc, ri=ri, pr=pr: e.matmul(pr[:, :], lhsT=wrot[:, kc, ri * 128:(ri + 1) * 128], rhs=hT[:, kc, :],
                                                                            start=(kc == 0), stop=(kc == 7)),
                              reads=hnames + wnames, writes=[prn])
                    ci = 0 if oc < 4 else 2
                    a1 = r1[ri % 2]
                    a2 = r2[ri % 2]
                    sc.op("dve", lambda e, pq=pq, a1=a1, tb=tb, ci=ci: e.tensor_tensor(out=a1[:, :], in0=pq[:, :], in1=tb[:, ci, :], op=ALU.mult),
                          reads=[pqn] + tp, writes=["r1_%d" % (ri % 2)])
                    sc.op("dve", lambda e, pr=pr, a2=a2, tb=tb, ci=ci: e.tensor_tensor(out=a2[:, :], in0=pr[:, :], in1=tb[:, ci + 1, :], op=ALU.mult),
                          reads=[prn] + tp, writes=["r2_%d" % (ri % 2)])
                    sc.op("dve", lambda e, a1=a1, a2=a2, o_=o_: e.tensor_tensor(out=o_[:, :], in0=a1[:, :], in1=a2[:, :], op=ALU.add),
                          reads=["r1_%d" % (ri % 2), "r2_%d" % (ri % 2)], writes=[on])
                else:
                    sc.op("act", lambda e, pq=pq, o_=o_: e.activation(out=o_[:, :], in_=pq[:, :], func=AF.Copy),
                          reads=[pqn], writes=[on])
                sc.op("sync", lambda e, o_=o_, oc=oc, t0=t0: e.dma_start(out=qkvT[oc * 128:(oc + 1) * 128, t0:t0 + 512], in_=o_[:, :]),
                      reads=[on], writes=["qkvT"], slot=on)
        sc.emit()
    return nc


def _pc(v, n):
    return np.ascontiguousarray(np.asarray(v, np.float32).reshape(n, 128).T)


def rope_tables():
    out = {}
    pos = np.arange(S, dtype=np.float32)
    for name, half in (("d", 16), ("l", 32)):
        inv = (np.float32(10000.0) ** (-(np.arange(half, dtype=np.float32)) / np.float32(half))).astype(np.float32)
        ang = (pos[None, :] * inv[:, None]).astype(np.float32)
        cos = np.cos(ang.astype(np.float64)).astype(np.float32)
        sin = np.sin(ang.astype(np.float64)).astype(np.float32)
        reps = 128 // half
        out["cos" + name] = np.tile(cos, (reps, 1))
        out["sin" + name] = np.tile(sin, (reps, 1))
    return out


_ROPE = None


def prep_a(inp, l, xT_cores):
    global _ROPE
    if _ROPE is None:
        _ROPE = rope_tables()
    maps = []
    for core in range(NCORE):
        b, q = divmod(core, 4)
        t0 = q * TQ
        m = {
            "xT": xT_cores[core],
            "cT": _pc(inp["c"][b], 8),
            "wada": np.ascontiguousarray(inp["w_ada"][l][:, 0:2048]),
            "bada": _pc(inp["b_ada"][l][0:2048], 16),
            "gat": _pc(inp["g_attn"][l], 8),
            "win": np.ascontiguousarray(inp["w_in"][l]),
            "cosd": np.ascontiguousarray(_ROPE["cosd"][:, t0:t0 + TQ]),
            "sind": np.ascontiguousarray(_ROPE["sind"][:, t0:t0 + TQ]),
            "cosl": np.ascontiguousarray(_ROPE["cosl"][:, t0:t0 + TQ]),
            "sinl": np.ascontiguousarray(_ROPE["sinl"][:, t0:t0 + TQ]),
        }
        maps.append(m)
    return maps


DILS = (1, 4, 16)
DUMMY_N = 512
VW = 128


def dil_geom(dil):
    lc = TQ // dil
    kcols = dil * (lc + 128)
    ntile = dil * (lc // 128 + 1)
    return lc, kcols, ntile


def build_stage_b():
    nc = new_nc()
    dq = nc.dram_tensor("dq", [4, 64, TQ], BF16, kind="ExternalInput").ap()
    dk = nc.dram_tensor("dk", [4, 64, S], BF16, kind="ExternalInput").ap()
    dv = nc.dram_tensor("dv", [4, 128, 128 * VW], BF16, kind="ExternalInput").ap()
    dlam = nc.dram_tensor("dlam", [64, 128], F32, kind="ExternalInput").ap()
    dcon = nc.dram_tensor("dcon", [64, 4], F32, kind="ExternalInput").ap()
    selh = nc.dram_tensor("selh", [128, 64], F32, kind="ExternalInput").ap()
    lq, lk, lv = {}, {}, {}
    for dil in DILS:
        lc, kcols, ntile = dil_geom(dil)
        lq[dil] = nc.dram_tensor("lq%d" % dil, [6, 64, TQ], BF16, kind="ExternalInput").ap()
        lk[dil] = nc.dram_tensor("lk%d" % dil, [6, 64, kcols], BF16, kind="ExternalInput").ap()
        lv[dil] = nc.dram_tensor("lv%d" % dil, [6, 128, ntile * VW], BF16, kind="ExternalInput").ap()
    lmask = nc.dram_tensor("lmask", [128, 4 * 1024], BF16, kind="ExternalInput").ap()
    identh = nc.dram_tensor("identh", [128, 128], BF16, kind="ExternalInput").ap()
    nq = nc.dram_tensor("nq", [6, 64, TQ], BF16, kind="ExternalInput").ap()
    nk = nc.dram_tensor("nk", [6, 64, 71 * 64], BF16, kind="ExternalInput").ap()
    nv = nc.dram_tensor("nv", [6, 64, 71 * VW], BF16, kind="ExternalInput").ap()
    nbias = nc.dram_tensor("nbias", [6, 64, 8 * 512], F32, kind="ExternalInput").ap()
    nmask = nc.dram_tensor("nmask", [64, 8 * 512], F32, kind="ExternalInput").ap()
    oT = nc.dram_tensor("oT", [D, TQ], BF16, kind="ExternalOutput").ap()

    sc = Sched(nc)
    OP = sc.op
    with contextlib.ExitStack() as es:
        def sb(name, shape, dt):
            return es.enter_context(nc.sbuf_tensor(name, shape, dt))

        def ps(name, shape, dt=F32):
            return es.enter_context(nc.psum_tensor(name, shape, dt))

        KH = [sb("KH%d" % i, [128, 8192], BF16) for i in range(2)]
        VH = [sb("VH%d" % i, [128, 71 * VW], BF16) for i in range(2)]
        QB = [sb("QB%d" % i, [64, TQ], BF16) for i in range(2)]
        PT = [sb("PT%d" % i, [128, 1024], BF16) for i in range(2)]
        A01 = [sb("A%d" % i, [128, 512], F32) for i in range(2)]
        W = [sb("W%d" % i, [64, 512], F32) for i in range(6)]
        OB = [sb("OB%d" % i, [64, 512], BF16) for i in range(2)]
        accs = sb("accs", [128, TQ], F32)
        mk = sb("mk", [128, 4 * 1024], BF16)
        ident = sb("ident", [128, 128], BF16)
        EE = [sb("E%d" % i, [64, 8 * 512], F32) for i in range(2)]
        nm = sb("nm", [64, 8 * 512], F32)
        XS = [sb("XS%d" % i, [64, 512], F32) for i in range(2)]
        cst = sb("cst", [128, 256], F32)
        sm = sb("sm", [64, 256], F32)
        sS = [ps("sS%d" % i, [128, 1024]) for i in range(2)]
        acc = [ps("acc%d" % i, [128, 512]) for i in range(2)]
        post = [ps("post%d" % i, [128, 512]) for i in range(2)]

        sel = cst[0:128, 0:64]
        ones64 = cst[0:64, 64:128]
        eps_ap = cst[0:64, 128:129]
        OP("pool", lambda e: e.memset(cst[:, :], 0.0), writes=["sel", "ones64", "epsb"])
        OP("sync", lambda e: e.dma_start(out=sel, in_=selh), writes=["sel"], slot="sel")
        OP("pool", lambda e: e.memset(ones64, 1.0 / 64.0), writes=["ones64"])
        OP("pool", lambda e: e.memset(eps_ap, EPS), writes=["epsb"])
        OP("sync", lambda e: e.dma_start(out=sm[:, 0:128], in_=dlam), writes=["dlam"], slot="dlam")
        OP("sync", lambda e: e.dma_start(out=sm[:, 128:132], in_=dcon), writes=["dcon"], slot="dcon")
        OP("poolq", lambda e: e.dma_start(out=mk[:, :], in_=lmask), writes=["mk"], slot="mk")
        OP("poolq", lambda e: e.dma_start(out=ident[:, :], in_=identh), writes=["ident"], slot="ident")
        OP("poolq", lambda e: e.dma_start(out=nm[:, :], in_=nmask), writes=["nm"], slot="nm")
        dl = sm[:, 0:128].rearrange("p (a b d) -> p a b d", a=2, b=2)
        pr = sm[:, 136:200].rearrange("p (a d) -> p a d", a=2)
        OP("dve", lambda e: e.tensor_tensor(out=pr, in0=dl[:, :, 0, :], in1=dl[:, :, 1, :], op=ALU.mult),
           reads=["dlam"], writes=["lprod"])
        OP("dve", lambda e: e.reduce_sum(out=sm[:, 200:202], in_=pr, axis=mybir.AxisListType.X),
           reads=["lprod"], writes=["lsum"])
        OP("act", lambda e: e.activation(out=sm[:, 202:204], in_=sm[:, 200:202], func=AF.Exp), reads=["lsum"], writes=["lexp"])
        OP("dve", lambda e: e.scalar_tensor_tensor(out=sm[:, 204:205], in0=sm[:, 203:204], scalar=sm[:, 202:203], in1=sm[:, 128:129],
                                                   op0=ALU.subtract, op1=ALU.subtract),
           reads=["lexp", "dcon"], writes=["nlam"])
        OP("dve", lambda e: e.tensor_tensor(out=sm[:, 205:206], in0=sm[:, 130:131], in1=sm[:, 129:130], op=ALU.mult),
           reads=["dcon"], writes=["gsub"])
        nlam = sm[:, 204:205]
        gsub = sm[:, 205:206]

        oi = [0]

        def finalize(src_o, src_names, rows, qcols, nrm=None):
            o_ = OB[oi[0] % 2]
            on = "OB%d" % (oi[0] % 2)
            oi[0] += 1
            OP("pool", lambda e, o_=o_, src_o=src_o: e.tensor_copy(out=o_[:, :], in_=src_o), reads=src_names, writes=[on])
            OP("poolq", lambda e, o_=o_, rows=rows, qcols=qcols: e.dma_start(out=oT[rows:rows + 64, qcols:qcols + 512], in_=o_[:, :]),
               reads=[on], writes=["oT"], slot=on)

        def swap_recip(src, src_names, pb, wi):
            OP("pe", lambda e, src=src, pb=pb: e.matmul(post[pb][0:64, :], lhsT=sel, rhs=src[0:VW, :], start=True, stop=True),
               reads=src_names + ["sel"], writes=["post%d" % pb])
            OP("dve", lambda e, pb=pb, wi=wi: e.reciprocal(out=W[wi][:, :], in_=post[pb][0:64, :]),
               reads=["post%d" % pb], writes=["W%d" % wi])

        SCL_D = 1.0 / math.sqrt(32.0)
        VHv = [VH[i][:, 0:64 * VW].rearrange("p (t w) -> p t w", w=VW) for i in range(2)]
        step = 0
        pending = []

        def diff_post1(h, q0):
            for m in range(2):
                swap_recip(A01[m], ["A%d" % m], 0, m)
            OP("dve", lambda e: e.tensor_tensor(out=W[2][:, :], in0=A01[0][0:64, :], in1=W[0][:, :], op=ALU.mult),
               reads=["A0", "W0"], writes=["W2"])
            OP("dve", lambda e: e.tensor_tensor(out=W[3][:, :], in0=A01[1][0:64, :], in1=W[1][:, :], op=ALU.mult),
               reads=["A1", "W1"], writes=["W3"])
            OP("dve", lambda e: e.scalar_tensor_tensor(out=W[4][:, :], in0=W[3][:, :], scalar=nlam, in1=W[2][:, :],
                                                       op0=ALU.mult, op1=ALU.add),
               reads=["W2", "W3", "nlam"], writes=["W4"])
            OP("dve", lambda e: e.tensor_tensor(out=W[5][:, :], in0=W[4][:, :], in1=W[4][:, :], op=ALU.mult),
               reads=["W4"], writes=["W5"])

        def diff_post2(h, q0):
            OP("pe", lambda e: e.matmul(post[0][0:64, :], lhsT=ones64, rhs=W[5][:, :], start=True, stop=True),
               reads=["W5", "ones64"], writes=["post0"])
            OP("act", lambda e: e.activation(out=W[0][:, :], in_=post[0][0:64, :], func=AF.Sqrt, bias=eps_ap, scale=1.0),
               reads=["post0", "epsb"], writes=["W0"])
            OP("dve", lambda e: e.reciprocal(out=W[1][:, :], in_=W[0][:, :]), reads=["W0"], writes=["W1"])
            OP("dve", lambda e: e.scalar_tensor_tensor(out=W[2][:, :], in0=W[4][:, :], scalar=gsub, in1=W[1][:, :],
                                                       op0=ALU.mult, op1=ALU.mult),
               reads=["W4", "W1", "gsub"], writes=["W2"])
            finalize(W[2][:, :], ["W2"], 64 * h, q0)

        for h in range(4):
            for hf in range(2):
                OP("sync", lambda e, h=h, hf=hf: e.dma_start(out=KH[hf][0:64, :], in_=dk[h, :, hf * 8192:(hf + 1) * 8192]),
                   writes=["KH%d" % hf], slot="KH%d" % hf)
                OP("sync", lambda e, h=h, hf=hf: e.dma_start(out=VH[hf][:, 0:64 * VW], in_=dv[h, :, hf * 64 * VW:(hf + 1) * 64 * VW]),
                   writes=["VH%d" % hf], slot="VH%d" % hf)
            qb = QB[h % 2]
            qn = "QB%d" % (h % 2)
            OP("poolq", lambda e, h=h, qb=qb: e.dma_start(out=qb[:, :], in_=dq[h, :, :]), writes=[qn], slot=qn)
            for qt in range(NBLK):
                q0 = qt * 512

                def s_mm(kt, step, qb=qb, q0=q0, qn=qn):
                    hf, kk = divmod(kt, 64)
                    s_ = sS[step % 2]
                    sn = "sS%d" % (step % 2)
                    for m in range(2):
                        OP("pe", lambda e, m=m, hf=hf, kk=kk, s_=s_, qb=qb, q0=q0: e.matmul(
                            s_[:, m * 512:(m + 1) * 512], lhsT=KH[hf][32 * m:32 * m + 32, kk * 128:(kk + 1) * 128],
                            rhs=qb[32 * m:32 * m + 32, q0:q0 + 512], start=True, stop=True),
                           reads=["KH%d" % hf, qn], writes=[sn])

                s_mm(0, step)
                s_mm(1, step + 1)
                for kt in range(128):
                    hf, kk = divmod(kt, 64)
                    s_ = sS[step % 2]
                    sn = "sS%d" % (step % 2)
                    p_ = PT[step % 2]
                    pn = "PT%d" % (step % 2)
                    OP("act", lambda e, s_=s_, p_=p_: e.activation(out=p_[:, :], in_=s_[:, :], func=AF.Exp, scale=SCL_D),
                       reads=[sn], writes=[pn])
                    if kt + 2 < 128:
                        s_mm(kt + 2, step + 2)
                    for m in range(2):
                        OP("pe", lambda e, m=m, hf=hf, kk=kk, p_=p_, kt=kt: e.matmul(
                            acc[m][:, :], lhsT=VHv[hf][:, kk, :], rhs=p_[:, m * 512:(m + 1) * 512],
                            start=(kt == 0), stop=(kt == 127)),
                           reads=["VH%d" % hf, pn], writes=["acc%d" % m])
                    if DUMMY_N:
                        OP("pe", lambda e, hf=hf, kk=kk, p_=p_: e.matmul(post[1][:, 0:DUMMY_N], lhsT=VHv[hf][:, kk, :], rhs=p_[:, 0:DUMMY_N], start=True, stop=True),
                           reads=["VH%d" % hf, pn], writes=["dummy"])
                    step += 1
                    if kt in (6, 48) and pending:
                        pending.pop(0)()
                for m in range(2):
                    OP("act", lambda e, m=m: e.activation(out=A01[m][:, :], in_=acc[m][:, :], func=AF.Copy),
                       reads=["acc%d" % m], writes=["A%d" % m])
                pending.append(lambda h=h, q0=q0: diff_post1(h, q0))
                pending.append(lambda h=h, q0=q0: diff_post2(h, q0))
        while pending:
            pending.pop(0)()

        SCL = 0.125
        it = 0
        COMBOS = {1: [0, 1, 1, 1, 1, 1, 1, 2], 4: [0, 2] * 4, 16: [3] * 8}
        items = []
        for h in range(6):
            for pi, dil in enumerate(DILS):
                for g in range(8):
                    items.append((h, pi, dil, g, it % 2))
                it += 1

        def dil_s1(item, stp):
            h, pi, dil, g, bi = item
            lc, kcols, ntile = dil_geom(dil)
            tpc = lc // 128
            kn, vn, qn = "KH%d" % bi, "VH%d" % bi, "QB%d" % bi
            if g == 0:
                OP("sync", lambda e: e.dma_start(out=KH[bi][0:64, 0:kcols], in_=lk[dil][h, :, :]), writes=[kn], slot=kn)
                OP("sync", lambda e: e.dma_start(out=VH[bi][:, 0:ntile * VW], in_=lv[dil][h, :, :]), writes=[vn], slot=vn)
                OP("sync", lambda e: e.dma_start(out=QB[bi][:, :], in_=lq[dil][h, :, :]), writes=[qn], slot=qn)
            s_ = sS[stp % 2]
            sn = "sS%d" % (stp % 2)
            for tl in range(4):
                tau = g * 4 + tl
                rho, j = divmod(tau, tpc)
                kc0 = rho * (lc + 128) + 128 * j
                for ab in range(2):
                    OP("pe", lambda e, tl=tl, ab=ab, kc0=kc0, tau=tau: e.matmul(
                        s_[:, tl * 256 + ab * 128: tl * 256 + ab * 128 + 128],
                        lhsT=KH[bi][0:64, kc0 + ab * 128: kc0 + ab * 128 + 128],
                        rhs=QB[bi][:, tau * 128:(tau + 1) * 128], start=True, stop=True),
                       reads=[kn, qn], writes=[sn])

        def dil_s2(item, stp):
            h, pi, dil, g, bi = item
            lc, kcols, ntile = dil_geom(dil)
            tpc = lc // 128
            kn, vn, qn = "KH%d" % bi, "VH%d" % bi, "QB%d" % bi
            vv = VH[bi][:, 0:ntile * VW].rearrange("p (t w) -> p t w", w=VW)
            s_ = sS[stp % 2]
            sn = "sS%d" % (stp % 2)
            p_ = PT[stp % 2]
            pn = "PT%d" % (stp % 2)
            a_ = acc[stp % 2]
            an = "acc%d" % (stp % 2)
            OP("act", lambda e: e.activation(out=p_[:, :], in_=s_[:, :], func=AF.Exp, scale=SCL), reads=[sn], writes=[pn])
            cb = COMBOS[dil][g]
            OP("pool", lambda e: e.tensor_tensor(out=p_[:, :], in0=p_[:, :], in1=mk[:, cb * 1024:(cb + 1) * 1024], op=ALU.mult),
               reads=[pn, "mk"], writes=[pn])
            for tl in range(4):
                tau = g * 4 + tl
                rho, j = divmod(tau, tpc)
                vt0 = rho * (tpc + 1) + j
                for ab in range(2):
                    OP("pe", lambda e, tl=tl, ab=ab, vt0=vt0: e.matmul(
                        a_[:, tl * 128:(tl + 1) * 128], lhsT=vv[:, vt0 + ab, :],
                        rhs=p_[:, tl * 256 + ab * 128: tl * 256 + ab * 128 + 128], start=(ab == 0), stop=(ab == 1)),
                       reads=[vn, pn], writes=[an])
            if dil == 1:
                dst = accs[:, g * 512:(g + 1) * 512]
                src = a_[:, :]
            elif dil == 4:
                rho, half = divmod(g, 2)
                dst = accs[:, :].rearrange("p (i r) -> p r i", r=4)[:, rho, half * 512:(half + 1) * 512]
                src = a_[:, :]
            else:
                dst = accs[:, :].rearrange("p (i r) -> p r i", r=16)[:, 2 * g:2 * g + 2, :]
                src = a_[:, :].rearrange("p (r i) -> p r i", r=2)
            if pi == 0:
                OP("dve", lambda e: e.tensor_copy(out=dst, in_=src), reads=[an], writes=["accs"])
            else:
                OP("dve", lambda e: e.tensor_tensor(out=dst, in0=dst, in1=src, op=ALU.add), reads=[an, "accs"], writes=["accs"])
            if pi == 2 and g == 7:
                for qt in range(NBLK):
                    src2 = accs[:, qt * 512:(qt + 1) * 512]
                    swap_recip(src2, ["accs"], 0, qt % 2)
                    OP("dve", lambda e, qt=qt, src2=src2: e.tensor_tensor(out=W[2 + qt % 2][:, :], in0=src2[0:64, :], in1=W[qt % 2][:, :], op=ALU.mult),
                       reads=["accs", "W%d" % (qt % 2)], writes=["W%d" % (2 + qt % 2)])
                    finalize(W[2 + qt % 2][:, :], ["W%d" % (2 + qt % 2)], 256 + 64 * h, qt * 512)

        def run_pipeline(items, s1, s2, step):
            for i, item in enumerate(items):
                if i == 0:
                    s1(item, step)
                if i + 1 < len(items):
                    s1(items[i + 1], step + 1)
                s2(item, step)
                step += 1
            return step

        step = run_pipeline(items, dil_s1, dil_s2, step)

        items = []
        for h in range(6):
            for rl in range(64):
                items.append((h, rl, it % 2))
            it += 1

        def na_s1(item, stp):
            h, rl, bi = item
            kn, vn, qn = "KH%d" % bi, "VH%d" % bi, "QB%d" % bi
            if rl == 0:
                OP("sync", lambda e: e.dma_start(out=KH[bi][0:64, 0:71 * 64], in_=nk[h, :, :]), writes=[kn], slot=kn)
                OP("sync", lambda e: e.dma_start(out=VH[bi][0:64, 0:71 * VW], in_=nv[h, :, :]), writes=[vn], slot=vn)
                OP("sync", lambda e: e.dma_start(out=QB[bi][:, :], in_=nq[h, :, :]), writes=[qn], slot=qn)
                Eb = EE[h % 2]
                en = "E%d" % (h % 2)
                OP("sync", lambda e: e.dma_start(out=Eb[:, :], in_=nbias[h, :, :]), writes=[en], slot=en)
                OP("act", lambda e: e.activation(out=Eb[:, :], in_=Eb[:, :], func=AF.Exp), reads=[en], writes=[en])
                OP("pool", lambda e: e.tensor_tensor(out=Eb[:, :], in0=Eb[:, :], in1=nm[:, :], op=ALU.mult), reads=[en, "nm"], writes=[en])
            s_ = sS[stp % 2]
            sn = "sS%d" % (stp % 2)
            for j in range(8):
                OP("pe", lambda e, j=j: e.matmul(
                    s_[0:64, j * 64:(j + 1) * 64], lhsT=KH[bi][0:64, (rl + j) * 64:(rl + j + 1) * 64],
                    rhs=QB[bi][:, rl * 64:(rl + 1) * 64], start=True, stop=True),
                   reads=[kn, qn], writes=[sn])

        def na_s2(item, stp):
            h, rl, bi = item
            kn, vn, qn = "KH%d" % bi, "VH%d" % bi, "QB%d" % bi
            Eb = EE[h % 2]
            en = "E%d" % (h % 2)
            vv = VH[bi][0:64, 0:71 * VW].rearrange("p (t w) -> p t w", w=VW)
            rg, rr = divmod(rl, 8)
            a_ = acc[rg % 2]
            an = "acc%d" % (rg % 2)
            var = rl + 1 if rl < 4 else (rl - 61 + 5 if rl >= 61 else 0)
            s_ = sS[stp % 2]
            sn = "sS%d" % (stp % 2)
            p_ = PT[stp % 2]
            pn = "PT%d" % (stp % 2)
            x_ = XS[stp % 2]
            xn = "XS%d" % (stp % 2)
            OP("act", lambda e: e.activation(out=x_[:, :], in_=s_[0:64, 0:512], func=AF.Exp, scale=SCL), reads=[sn], writes=[xn])
            OP("dve", lambda e: e.tensor_tensor(out=p_[0:64, 0:512], in0=x_[:, :], in1=Eb[:, var * 512:(var + 1) * 512], op=ALU.mult),
               reads=[xn, en], writes=[pn])
            for j in range(8):
                OP("pe", lambda e, j=j: e.matmul(
                    a_[:, rr * 64:(rr + 1) * 64], lhsT=vv[:, rl + j, :], rhs=p_[0:64, j * 64:(j + 1) * 64],
                    start=(j == 0), stop=(j == 7)),
                   reads=[vn, pn], writes=[an])
            if rr == 7:
                ai = rg % 2
                OP("act", lambda e: e.activation(out=A01[ai][:, :], in_=a_[:, :], func=AF.Copy), reads=[an], writes=["A%d" % ai])
                swap_recip(A01[ai], ["A%d" % ai], 0, ai)
                OP("dve", lambda e: e.tensor_tensor(out=W[2 + ai][:, :], in0=A01[ai][0:64, :], in1=W[ai][:, :], op=ALU.mult),
                   reads=["A%d" % ai, "W%d" % ai], writes=["W%d" % (2 + ai)])
                finalize(W[2 + ai][:, :], ["W%d" % (2 + ai)], 640 + 64 * h, rg * 512)

        step = run_pipeline(items, na_s1, na_s2, step)
        sc.emit()
    return nc


def _bf(a):
    return np.ascontiguousarray(a).view(NPBF) if a.dtype == np.uint16 else np.ascontiguousarray(a)


def _vaug(vT, key_idx):
    valid = (key_idx >= 0) & (key_idx < S)
    idx = np.clip(key_idx, 0, S - 1)
    v = vT.T[idx]
    v = np.where(valid[..., None], v, np.zeros((), v.dtype))
    ones = np.ones(key_idx.shape + (VW - 64,), v.dtype)
    return np.concatenate([v, ones], axis=-1)


def _kcols(kT, key_idx):
    valid = (key_idx >= 0) & (key_idx < S)
    idx = np.clip(key_idx, 0, S - 1)
    k = kT[:, idx]
    return np.where(valid[None, :], k, np.zeros((), k.dtype))


def dil_masks(q):
    a = np.arange(128)[:, None]
    b = np.arange(128)[None, :]
    GA = (a >= b).astype(np.float32)
    GB = (a <= b).astype(np.float32)
    G = np.concatenate([GA, GB], 1)
    FA = GA.copy()
    LB = GB.copy()
    if q == 0:
        FA[0:64, :] = 0
    if q == 3:
        LB[64:128, :] = 0
    F = np.concatenate([FA, GB], 1)
    L = np.concatenate([GA, LB], 1)
    combos = [[F, G, G, G], [G, G, G, G], [G, G, G, L], [F, L, F, L]]
    return np.concatenate([np.concatenate(c, 1) for c in combos], 1).astype(NPBF)


def na_tables(rpb, q):
    kc = np.arange(64)[:, None]
    qc = np.arange(64)[None, :]
    dc = np.clip(kc - qc, -15, 15) + 15
    cs = np.clip(qc - 8, 0, 48)
    cmask = ((kc >= cs) & (kc < cs + 16)).astype(np.float32)
    dr = np.zeros((8, 8), np.int64)
    for var in range(8):
        for j in range(8):
            if var == 0:
                d = j + 3
            elif var <= 4:
                rl = var - 1
                if q == 0:
                    bsl = rl + j
                    g = bsl - 4 if bsl >= 4 else bsl + 4
                    d = g - rl + 7
                else:
                    d = j + 3
            else:
                rl = 61 + (var - 5)
                if q == 3:
                    bsl = rl + j
                    g = 188 + bsl if bsl <= 67 else 248 + (bsl - 68)
                    d = g - (192 + rl) + 7
                else:
                    d = j + 3
            dr[var, j] = d
    tab = rpb[:, dr[None, :, :, None], dc[:, None, None, :]]
    mask = np.broadcast_to(cmask[:, None, None, :], (64, 8, 8, 64))
    return np.ascontiguousarray(tab.reshape(6, 64, 8 * 512), np.float32), np.ascontiguousarray(mask.reshape(64, 8 * 512), np.float32)


def na_buffer_rows(q):
    rows = np.arange(71) + 64 * q - 4
    if q == 0:
        rows = np.array([4, 5, 6, 7] + list(range(0, 67)))
    if q == 3:
        rows = np.array(list(range(188, 256)) + [248, 249, 250])
    return rows


def prep_b(inp, l, qkv_full):
    lam_init = 0.8 - 0.6 * math.exp(-0.3 * l)
    selh = np.zeros((128, 64), np.float32)
    for i in range(64):
        selh[64 + i, i] = 1.0
    maps = []
    for core in range(NCORE):
        b, q = divmod(core, 4)
        t0 = q * TQ
        full = qkv_full[b]
        qa, ka, va = full[0:256], full[256:512], full[512:768]
        qb, kb, vb = full[768:1152], full[1152:1536], full[1536:1920]
        qc, kc, vc = full[1920:2304], full[2304:2688], full[2688:3072]
        m = {}
        m["dq"] = np.ascontiguousarray(qa[:, t0:t0 + TQ].reshape(4, 64, TQ))
        m["dk"] = np.ascontiguousarray(ka.reshape(4, 64, S))
        keys = (np.arange(128)[None, :] * 128 + np.arange(128)[:, None])
        m["dv"] = np.stack([_vaug(va[64 * h:64 * h + 64], keys).reshape(128, 128 * VW) for h in range(4)])
        m["dlam"] = np.ascontiguousarray(np.broadcast_to(inp["diff_lambda"][l].reshape(1, 128), (64, 128)), np.float32)
        dcon = np.zeros((64, 4), np.float32)
        dcon[:, 0] = lam_init
        dcon[:, 1] = 1.0 - lam_init
        dcon[:, 2] = inp["diff_subln"][l]
        m["dcon"] = dcon
        m["selh"] = selh
        for dil in DILS:
            lc, kcols, ntile = dil_geom(dil)
            tpc = lc // 128
            i0 = t0 // dil
            loc = (np.arange(lc)[None, :] * dil + np.arange(dil)[:, None]).reshape(-1)
            m["lq%d" % dil] = np.ascontiguousarray(qb[:, t0 + loc].reshape(6, 64, TQ))
            kidx = ((i0 - 64 + np.arange(lc + 128))[None, :] * dil + np.arange(dil)[:, None])
            kidx = np.where((i0 - 64 + np.arange(lc + 128))[None, :] < 0, -1, kidx)
            kidx = np.where((i0 - 64 + np.arange(lc + 128))[None, :] >= S // dil, -1, kidx).reshape(-1)
            m["lk%d" % dil] = np.stack([_kcols(kb[64 * h:64 * h + 64], kidx) for h in range(6)])
            ci = i0 - 64 + 128 * np.arange(tpc + 1)[None, None, :] + np.arange(128)[:, None, None]
            tok = ci * dil + np.arange(dil)[None, :, None]
            tok = np.where((ci < 0) | (ci >= S // dil), -1, tok).reshape(128, ntile)
            m["lv%d" % dil] = np.stack([_vaug(vb[64 * h:64 * h + 64], tok).reshape(128, ntile * VW) for h in range(6)])
        m["lmask"] = dil_masks(q)
        m["identh"] = np.eye(128, dtype=np.float32).astype(NPBF)
        rows = na_buffer_rows(q)
        tokn = (rows[:, None] * 64 + np.arange(64)[None, :])
        m["nq"] = np.ascontiguousarray(qc[:, t0:t0 + TQ].reshape(6, 64, TQ))
        m["nk"] = np.stack([np.ascontiguousarray(kc[64 * h:64 * h + 64][:, tokn.reshape(-1)]) for h in range(6)])
        m["nv"] = np.stack([_vaug(vc[64 * h:64 * h + 64], tokn.T).reshape(64, 71 * VW) for h in range(6)])
        tab, mask = na_tables(inp["na_rpb"][l], q)
        m["nbias"] = tab
        m["nmask"] = mask
        maps.append(m)
    return maps


NFC = DFF // 128


def build_stage_c(final):
    nc = new_nc()
    xh = nc.dram_tensor("xh", [D, TQ + 2], F32, kind="ExternalInput").ap()
    oh = nc.dram_tensor("oh", [D, TQ + 2], BF16, kind="ExternalInput").ap()
    edge = nc.dram_tensor("edge", [128, 2], F32, kind="ExternalInput").ap()
    cT = nc.dram_tensor("cT", [128, 8], F32, kind="ExternalInput").ap()
    wada = nc.dram_tensor("wada", [D, 4096], F32, kind="ExternalInput").ap()
    bada = nc.dram_tensor("bada", [128, 32], F32, kind="ExternalInput").ap()
    gff = nc.dram_tensor("gff", [128, 8], F32, kind="ExternalInput").ap()
    gfin = nc.dram_tensor("gfin", [128, 8], F32, kind="ExternalInput").ap()
    wout = nc.dram_tensor("wout", [D, D], F32, kind="ExternalInput").ap()
    wup = nc.dram_tensor("wup", [D, 2 * DFF], F32, kind="ExternalInput").ap()
    convw = nc.dram_tensor("convw", [128, NFC * 3], F32, kind="ExternalInput").ap()
    convb = nc.dram_tensor("convb", [128, NFC], F32, kind="ExternalInput").ap()
    wdown = nc.dram_tensor("wdown", [DFF, D], F32, kind="ExternalInput").ap()
    xo = nc.dram_tensor("xo", [D, TQ], F32, kind="ExternalOutput").ap()
    wupb = nc.dram_tensor("wupb", [NFC, 128, 8 * 256], BF16).ap()

    sc = Sched(nc)
    OP = sc.op
    with contextlib.ExitStack() as es:
        def sb(name, shape, dt):
            return es.enter_context(nc.sbuf_tensor(name, shape, dt))

        def ps(name, shape, dt=F32):
            return es.enter_context(nc.psum_tensor(name, shape, dt))

        NW = 514
        woutb = sb("woutb", [128, 8, D], BF16)
        wdnb = sb("wdnb", [128, NFC, D], BF16)
        XB = [sb("XB%d" % i, [128, 8, NW], F32) for i in range(2)]
        OBk = [sb("OBk%d" % i, [128, 8, NW], BF16) for i in range(2)]
        xm = sb("xm", [128, 8, NW], F32)
        sq = [sb("sq%d" % i, [128, NW], F32) for i in range(2)]
        rstd = sb("rstd", [128, NW], F32)
        ht = [sb("ht%d" % i, [128, NW], F32) for i in range(2)]
        h2 = sb("h2", [128, 8, NW], BF16)
        slab = [sb("slab%d" % i, [128, 8, 256], BF16) for i in range(3)]
        cv = [sb("cv%d" % i, [128, 512], F32) for i in range(3)]
        sl = [sb("sl%d" % i, [128, 512], F32) for i in range(2)]
        prod = sb("prod", [128, NFC, 512], BF16)
        small = sb("small", [128, 256], F32)
        ones = sb("ones", [128, 128], F32)
        T2 = [ps("T2%d" % i, [128, 1024]) for i in range(3)]
        V = [ps("V%d" % i, [128, 512]) for i in range(2)]

        c_ap, silu_ap, mod_ap, bada_ap = small[:, 0:8], small[:, 8:16], small[:, 16:48], small[:, 48:80]
        gta, shf, scf, gtf = small[:, 16:24], small[:, 24:32], small[:, 32:40], small[:, 40:48]
        gff_ap, gscf, gfin_ap = small[:, 80:88], small[:, 88:96], small[:, 96:104]
        cw_ap, cb_ap, edge_ap, eps_ap = small[:, 104:170], small[:, 170:192], small[:, 192:194], small[:, 194:195]
        for nm_, ap_, src_ in (("cT", c_ap, cT), ("bada", bada_ap, bada), ("gff", gff_ap, gff), ("gfin", gfin_ap, gfin),
                               ("convw", cw_ap, convw), ("convb", cb_ap, convb), ("edge", edge_ap, edge)):
            OP("sync", lambda e, ap_=ap_, src_=src_: e.dma_start(out=ap_, in_=src_), writes=[nm_], slot=nm_)
        OP("pool", lambda e: e.memset(ones[:, :], 1.0 / D), writes=["ones"])
        OP("pool", lambda e: e.memset(eps_ap, EPS), writes=["epsc"])
        OP("act", lambda e: e.activation(out=silu_ap, in_=c_ap, func=AF.Silu), reads=["cT"], writes=["silu"])

        xparts = [["XB%dk%d" % (i, k) for k in range(8)] for i in range(2)]
        for jq in range(8):
            wa = XB[jq % 2]
            for kc in range(8):
                OP("sync" if kc % 2 == 0 else "poolq",
                   lambda e, kc=kc, wa=wa, jq=jq: e.dma_start(out=wa[:, kc, 0:512], in_=wada[kc * 128:(kc + 1) * 128, jq * 512:(jq + 1) * 512]),
                   writes=[xparts[jq % 2][kc]], slot="XB%d" % (jq % 2))
            for jj in range(4):
                j = jq * 4 + jj
                for kc in range(8):
                    OP("pe", lambda e, j=j, jj=jj, kc=kc, wa=wa: e.matmul(T2[0][:, j:j + 1], lhsT=wa[:, kc, jj * 128:(jj + 1) * 128],
                                                                         rhs=silu_ap[:, kc:kc + 1], start=(kc == 0), stop=(kc == 7)),
                       reads=xparts[jq % 2] + ["silu"], writes=["T20"])
        OP("dve", lambda e: e.tensor_tensor(out=mod_ap, in0=T2[0][:, 0:32], in1=bada_ap, op=ALU.add), reads=["T20", "bada"], writes=["mod"])
        OP("dve", lambda e: e.scalar_tensor_tensor(out=gscf, in0=scf, scalar=1.0, in1=gff_ap, op0=ALU.add, op1=ALU.mult),
           reads=["mod", "gff"], writes=["gscf"])

        stg = xm[:, :, :].rearrange("p a b -> p (a b)")
        for kc in range(8):
            OP("sync", lambda e, kc=kc: e.dma_start(out=stg[:, 0:D], in_=wout[kc * 128:(kc + 1) * 128, :]), writes=["stgA"], slot="stgA")
            OP("act", lambda e, kc=kc: e.activation(out=woutb[:, kc, :], in_=stg[:, 0:D], func=AF.Copy), reads=["stgA"], writes=["woutb"])
        for fc in range(NFC):
            half = "stgA" if fc % 2 == 0 else "stgB"
            off = 0 if fc % 2 == 0 else 1024
            OP("sync", lambda e, fc=fc, off=off: e.dma_start(out=stg[:, off:off + D], in_=wdown[fc * 128:(fc + 1) * 128, :]), writes=[half], slot=half)
            OP("act" if fc % 2 == 0 else "dve", lambda e, fc=fc, off=off: (e.activation(out=wdnb[:, fc, :], in_=stg[:, off:off + D], func=AF.Copy)
                                                                          if fc % 2 == 0 else e.tensor_copy(out=wdnb[:, fc, :], in_=stg[:, off:off + D])),
               reads=[half], writes=["wdnb"])
        wupv = wupb.rearrange("f p (k g c) -> p k g f c", k=8, g=2)
        prodf = prod[:, :, :].rearrange("p a b -> p (a b)")
        for kc in range(8):
            base = (kc % 2) * 2 * DFF
            stn = "stb%d" % (kc % 2)
            for g in range(2):
                xbuf = XB[g][:, :, :].rearrange("p a b -> p (a b)")
                OP("sync" if g == 0 else "poolq",
                   lambda e, kc=kc, g=g, xbuf=xbuf: e.dma_start(out=xbuf[:, 0:DFF], in_=wup[kc * 128:(kc + 1) * 128, g * DFF:(g + 1) * DFF]),
                   writes=xparts[g], slot="XB%d" % g)
                if g == 0:
                    OP("act", lambda e, xbuf=xbuf, base=base: e.activation(out=prodf[:, base:base + DFF], in_=xbuf[:, 0:DFF], func=AF.Copy),
                       reads=xparts[g], writes=[stn + "g"])
                else:
                    OP("dve", lambda e, xbuf=xbuf, base=base: e.tensor_copy(out=prodf[:, base + DFF:base + 2 * DFF], in_=xbuf[:, 0:DFF]),
                       reads=xparts[g], writes=[stn + "u"])
            for g in range(2):
                src = prodf[:, base + g * DFF:base + (g + 1) * DFF].rearrange("p (f c) -> p f c", f=NFC)
                OP("sync", lambda e, kc=kc, g=g, src=src: e.dma_start(out=wupv[:, kc, g], in_=src),
                   reads=[stn + "gu"[g]], writes=["wupb%d_%d" % (kc, g)], slot=stn + "gu"[g])
        wupn = ["wupb%d_%d" % (k, g) for k in range(8) for g in range(2)]

        slab_i = [0]
        slab_q = []

        def issue_slab(fc):
            i = slab_i[0] % 3
            slab_i[0] += 1
            OP("sync", lambda e, fc=fc, i=i: e.dma_start(out=slab[i][:, :, :].rearrange("p a b -> p (a b)"), in_=wupb[fc, :, :]),
               reads=wupn, writes=["slab%d" % i], slot="slab%d" % i)
            slab_q.append(i)

        def load_block(blk):
            x_ = XB[blk % 2]
            o_ = OBk[blk % 2]
            c0 = blk * 512
            for kc in range(8):
                OP("sync" if kc % 2 == 0 else "poolq",
                   lambda e, kc=kc, x_=x_, c0=c0: e.dma_start(out=x_[:, kc, :], in_=xh[kc * 128:(kc + 1) * 128, c0:c0 + NW]),
                   writes=[xparts[blk % 2][kc]], slot="XB%d" % (blk % 2))
            OP("poolq", lambda e, o_=o_, c0=c0: e.dma_start(out=o_[:, :, :], in_=oh.rearrange("(k p) t -> p k t", p=128)[:, :, c0:c0 + NW]),
               writes=["OBk%d" % (blk % 2)], slot="OBk%d" % (blk % 2))

        load_block(0)
        issue_slab(0)
        issue_slab(1)
        for blk in range(NBLK):
            x_ = XB[blk % 2]
            o_ = OBk[blk % 2]
            xp = xparts[blk % 2]
            on_ = "OBk%d" % (blk % 2)
            if blk + 1 < NBLK:
                load_block(blk + 1)
            for oc in range(8):
                t2 = T2[oc % 2]
                tn = "T2%d" % (oc % 2)
                for (c_lo, c_hi) in ((0, 512), (512, NW)):
                    for kc in range(8):
                        OP("pe", lambda e, oc=oc, kc=kc, t2=t2, o_=o_, c_lo=c_lo, c_hi=c_hi: e.matmul(
                            t2[:, c_lo:c_hi], lhsT=woutb[:, kc, oc * 128:(oc + 1) * 128], rhs=o_[:, kc, c_lo:c_hi],
                            start=(kc == 0), stop=(kc == 7)),
                           reads=[on_, "woutb"], writes=[tn])
                OP("dve", lambda e, oc=oc, t2=t2, x_=x_: e.scalar_tensor_tensor(out=xm[:, oc, :], in0=t2[:, 0:NW], scalar=gta[:, oc:oc + 1],
                                                                              in1=x_[:, oc, :], op0=ALU.mult, op1=ALU.add),
                   reads=[tn, "mod", "wdnb"] + xp, writes=["xm%d" % oc])
                s_ = sq[oc % 2]
                sn = "sq%d" % (oc % 2)
                OP("act", lambda e, oc=oc, s_=s_: e.activation(out=s_[:, :], in_=xm[:, oc, :], func=AF.Square), reads=["xm%d" % oc], writes=[sn])
                for (c_lo, c_hi) in ((0, 512), (512, NW)):
                    OP("pe", lambda e, oc=oc, s_=s_, c_lo=c_lo, c_hi=c_hi: e.matmul(T2[2][:, c_lo:c_hi], lhsT=ones[:, :], rhs=s_[:, c_lo:c_hi],
                                                                                  start=(oc == 0), stop=(oc == 7)),
                       reads=[sn, "ones"], writes=["T22" if c_lo == 0 else "T22b"])
            xmn = ["xm%d" % k for k in range(8)]
            OP("act", lambda e: e.activation(out=rstd[:, :], in_=T2[2][:, 0:NW], func=AF.Sqrt, bias=eps_ap, scale=1.0),
               reads=["T22", "T22b", "epsc"], writes=["rstd0"])
            OP("dve", lambda e: e.reciprocal(out=rstd[:, :], in_=rstd[:, :]), reads=["rstd0"], writes=["rstd"])
            for kc in range(8):
                h_ = ht[kc % 2]
                hn = "ht%d" % (kc % 2)
                OP("dve", lambda e, kc=kc, h_=h_: e.tensor_tensor(out=h_[:, :], in0=xm[:, kc, :], in1=rstd[:, :], op=ALU.mult),
                   reads=["xm%d" % kc, "rstd"], writes=[hn])
                OP("act", lambda e, kc=kc, h_=h_: e.activation(out=h2[:, kc, :], in_=h_[:, :], func=AF.Identity,
                                                               bias=shf[:, kc:kc + 1], scale=gscf[:, kc:kc + 1]),
                   reads=[hn, "mod", "gscf"], writes=["h2_%d" % kc])
            h2n = ["h2_%d" % k for k in range(8)]
            if blk == 0:
                OP("pool", lambda e: e.tensor_scalar(out=h2[:, :, 0:1], in0=h2[:, :, 0:1], scalar1=edge_ap[:, 0:1], scalar2=None, op0=ALU.mult),
                   reads=h2n + ["edge"], writes=h2n)
            if blk == NBLK - 1:
                OP("pool", lambda e: e.tensor_scalar(out=h2[:, :, NW - 1:NW], in0=h2[:, :, NW - 1:NW], scalar1=edge_ap[:, 1:2], scalar2=None, op0=ALU.mult),
                   reads=h2n + ["edge"], writes=h2n)
            for fc in range(NFC):
                nxt = blk * NFC + fc + 2
                if nxt < NBLK * NFC:
                    issue_slab(nxt % NFC)
                si = slab_q.pop(0)
                sl_ = slab[si]
                sln = "slab%d" % si
                pg = T2[fc % 2]
                pgn = "T2%d" % (fc % 2)
                pu = V[fc % 2]
                pun = "V%d" % (fc % 2)
                for (c_lo, c_hi) in ((0, 512), (512, NW)):
                    for kc in range(8):
                        OP("pe", lambda e, kc=kc, pg=pg, sl_=sl_, c_lo=c_lo, c_hi=c_hi: e.matmul(
                            pg[:, c_lo:c_hi], lhsT=sl_[:, kc, 0:128], rhs=h2[:, kc, c_lo:c_hi], start=(kc == 0), stop=(kc == 7)),
                           reads=h2n + [sln], writes=[pgn])
                for kc in range(8):
                    OP("pe", lambda e, kc=kc, pu=pu, sl_=sl_: e.matmul(pu[:, :], lhsT=sl_[:, kc, 128:256], rhs=h2[:, kc, 1:513],
                                                                       start=(kc == 0), stop=(kc == 7)),
                       reads=h2n + [sln], writes=[pun])
                c_ = cv[fc % 3]
                cn = "cv%d" % (fc % 3)
                OP("dve", lambda e, fc=fc, pg=pg, c_=c_: e.tensor_scalar(out=c_[:, :], in0=pg[:, 0:512], scalar1=cw_ap[:, fc * 3:fc * 3 + 1], scalar2=None, op0=ALU.mult),
                   reads=[pgn, "convw"], writes=[cn])
                OP("dve", lambda e, fc=fc, pg=pg, c_=c_: e.scalar_tensor_tensor(out=c_[:, :], in0=pg[:, 1:513], scalar=cw_ap[:, fc * 3 + 1:fc * 3 + 2], in1=c_[:, :],
                                                                               op0=ALU.mult, op1=ALU.add),
                   reads=[pgn, "convw", cn], writes=[cn])
                OP("dve", lambda e, fc=fc, pg=pg, c_=c_: e.scalar_tensor_tensor(out=c_[:, :], in0=pg[:, 2:514], scalar=cw_ap[:, fc * 3 + 2:fc * 3 + 3], in1=c_[:, :],
                                                                               op0=ALU.mult, op1=ALU.add),
                   reads=[pgn, "convw", cn], writes=[cn])
                s2 = sl[fc % 2]
                s2n = "sl%d" % (fc % 2)
                OP("act", lambda e, fc=fc, c_=c_, s2=s2: e.activation(out=s2[:, :], in_=c_[:, :], func=AF.Silu, bias=cb_ap[:, fc:fc + 1], scale=1.0),
                   reads=[cn, "convb"], writes=[s2n])
                OP("dve", lambda e, fc=fc, s2=s2, pu=pu: e.tensor_tensor(out=prod[:, fc, :], in0=s2[:, :], in1=pu[:, :], op=ALU.mult),
                   reads=[s2n, pun], writes=["prod%d" % fc])
            prn = ["prod%d" % f for f in range(NFC)]
            for oc in range(8):
                pd = T2[2][:, (oc % 2) * 512:(oc % 2) * 512 + 512]
                pdn = "T22" if oc % 2 == 0 else "T22b"
                for fc in range(NFC):
                    OP("pe", lambda e, oc=oc, fc=fc, pd=pd: e.matmul(pd, lhsT=wdnb[:, fc, oc * 128:(oc + 1) * 128], rhs=prod[:, fc, :],
                                                                    start=(fc == 0), stop=(fc == NFC - 1)),
                       reads=prn + ["wdnb"], writes=[pdn])
                OP("dve", lambda e, oc=oc, pd=pd: e.scalar_tensor_tensor(out=xm[:, oc, 1:513], in0=pd, scalar=gtf[:, oc:oc + 1], in1=xm[:, oc, 1:513],
                                                                        op0=ALU.mult, op1=ALU.add),
                   reads=[pdn, "mod", "xm%d" % oc], writes=["xm%d" % oc])
            if final:
                for oc in range(8):
                    s_ = sq[oc % 2]
                    sn = "sq%d" % (oc % 2)
                    OP("act", lambda e, oc=oc, s_=s_: e.activation(out=s_[:, 0:512], in_=xm[:, oc, 1:513], func=AF.Square), reads=["xm%d" % oc], writes=[sn])
                    OP("pe", lambda e, oc=oc, s_=s_: e.matmul(T2[0][:, 0:512], lhsT=ones[:, :], rhs=s_[:, 0:512], start=(oc == 0), stop=(oc == 7)),
                       reads=[sn, "ones"], writes=["T20"])
                OP("act", lambda e: e.activation(out=rstd[:, 0:512], in_=T2[0][:, 0:512], func=AF.Sqrt, bias=eps_ap, scale=1.0),
                   reads=["T20", "epsc"], writes=["rstd0"])
                OP("dve", lambda e: e.reciprocal(out=rstd[:, 0:512], in_=rstd[:, 0:512]), reads=["rstd0"], writes=["rstd"])
                for oc in range(8):
                    OP("dve", lambda e, oc=oc: e.scalar_tensor_tensor(out=xm[:, oc, 1:513], in0=xm[:, oc, 1:513], scalar=gfin_ap[:, oc:oc + 1], in1=rstd[:, 0:512],
                                                                     op0=ALU.mult, op1=ALU.mult),
                       reads=["xm%d" % oc, "rstd", "gfin"], writes=["xm%d" % oc])
            for oc in range(8):
                OP("sync" if oc % 2 == 0 else "poolq",
                   lambda e, oc=oc, blk=blk: e.dma_start(out=xo[oc * 128:(oc + 1) * 128, blk * 512:(blk + 1) * 512], in_=xm[:, oc, 1:513]),
                   reads=["xm%d" % oc], writes=["xo"], slot="xm%d" % oc)
        sc.emit()
    return nc


def prep_c(inp, l, xT_cores, oT_cores):
    maps = []
    for core in range(NCORE):
        b, q = divmod(core, 4)

        def halo(arrs, dt):
            out = np.zeros((D, TQ + 2), dt)
            out[:, 1:TQ + 1] = arrs[core]
            if q > 0:
                out[:, 0] = arrs[core - 1][:, TQ - 1]
            if q < 3:
                out[:, TQ + 1] = arrs[core + 1][:, 0]
            return out

        edge = np.zeros((128, 2), np.float32)
        edge[:, 0] = 1.0 if q > 0 else 0.0
        edge[:, 1] = 1.0 if q < 3 else 0.0
        cw = np.ascontiguousarray(inp["conv_w"][l].reshape(3, NFC, 128).transpose(2, 1, 0).reshape(128, NFC * 3), np.float32)
        m = {
            "xh": halo(xT_cores, np.float32),
            "oh": halo(oT_cores, NPBF),
            "edge": edge,
            "cT": _pc(inp["c"][b], 8),
            "wada": np.ascontiguousarray(inp["w_ada"][l][:, 2048:6144]),
            "bada": _pc(inp["b_ada"][l][2048:6144], 32),
            "gff": _pc(inp["g_ffn"][l], 8),
            "gfin": _pc(inp["g_final"], 8),
            "wout": np.ascontiguousarray(inp["w_out"][l]),
            "wup": np.ascontiguousarray(inp["w_up"][l]),
            "convw": cw,
            "convb": _pc(inp["conv_b"][l], NFC),
            "wdown": np.ascontiguousarray(inp["w_down"][l]),
        }
        maps.append(m)
    return maps


_NC_CACHE = {}


def _get_nc(name):
    if name not in _NC_CACHE:
        if name == "a":
            _NC_CACHE[name] = build_stage_a()
        elif name == "b":
            _NC_CACHE[name] = build_stage_b()
        elif name == "c0":
            _NC_CACHE[name] = build_stage_c(False)
        else:
            _NC_CACHE[name] = build_stage_c(True)
    return _NC_CACHE[name]


def _run(name, maps):
    if name == "a":
        nc = build_stage_a()
    elif name == "b":
        nc = build_stage_b()
    else:
        nc = build_stage_c(name == "c1")
    return run_bass_kernel_spmd(nc, maps, core_ids=list(range(NCORE))).results


def kernel(**inp):
    inp = {k: np.asarray(v) for k, v in inp.items()}
    x = inp["x"]
    xT = [np.ascontiguousarray(x[c // 4, (c % 4) * TQ:(c % 4 + 1) * TQ, :].T) for c in range(NCORE)]
    for l in range(DEPTH):
        ra = _run("a", prep_a(inp, l, xT))
        qkv = [np.asarray(r["qkvT"]) for r in ra]
        full = [np.concatenate([qkv[b * 4 + q] for q in range(4)], axis=1) for b in range(B)]
        rb = _run("b", prep_b(inp, l, full))
        oT = [np.asarray(r["oT"]) for r in rb]
        rc = _run("c1" if l == DEPTH - 1 else "c0", prep_c(inp, l, xT, oT))
        xT = [np.asarray(r["xo"]) for r in rc]
    out = np.empty((B, S, D), np.float32)
    for c in range(NCORE):
        out[c // 4, (c % 4) * TQ:(c % 4 + 1) * TQ, :] = xT[c].T
    return out
```
